# Optimizing a Trainium2 kernel written in Bass

```python
import jax, jax.numpy as jnp
from jax import lax
import numpy as np

D_MODEL = 1024
BATCH = 4
SEQ = 8192
DEPTH = 2

HEAD_DIM = 64
DSW_GROUPS = ((128, 1), (512, 4), (2048, 16))
DSW_HEADS_PER_GROUP = 4
DSW_HEADS = DSW_HEADS_PER_GROUP * len(DSW_GROUPS)
SB_HEADS = D_MODEL // (4 * HEAD_DIM)
W_A = DSW_HEADS * HEAD_DIM
W_B = SB_HEADS * HEAD_DIM
OUT_A = DSW_HEADS_PER_GROUP * HEAD_DIM
OUT_B = W_B
N_IN = 3 * W_A + 3 * W_B + 2 * D_MODEL
BLOCK = 128
D_FF = 2816
CONV_WIDTH = 3
RMS_EPS = 1e-6

kernel_name = "hybrid_dilated_stickbreaking_gated_block"


def rmsnorm(x, g):
    xf = x.astype(jnp.float32)
    y = xf * lax.rsqrt(jnp.mean(xf * xf, axis=-1, keepdims=True) + RMS_EPS)
    return (y * g.astype(jnp.float32)).astype(x.dtype)


def alibi_slopes(n):
    return jnp.asarray(2.0 ** (-8.0 * np.arange(1, n + 1) / n), dtype=jnp.float32)


def dilated_window_attention(q, k, v, slopes, window, dilation):
    B, S, H, hd = q.shape
    W = window // dilation
    L = S // dilation
    nb = -(-L // W)
    Lp = nb * W

    def to_sub(a):
        a = a.reshape(B, L, dilation, H, hd).transpose(0, 2, 3, 1, 4)
        return jnp.pad(a, ((0, 0), (0, 0), (0, 0), (0, Lp - L), (0, 0)))

    def windows(a):
        ap = jnp.pad(to_sub(a), ((0, 0), (0, 0), (0, 0), (W, 0), (0, 0))).reshape(B, dilation, H, nb + 1, W, hd)
        return jnp.concatenate([ap[:, :, :, :-1], ap[:, :, :, 1:]], axis=-2)

    qs = to_sub(q).reshape(B, dilation, H, nb, W, hd)
    kw = windows(k)
    vw = windows(v).astype(jnp.float32)
    scores = jnp.einsum('brhnqc,brhnkc->brhnqk', qs, kw).astype(jnp.float32) * (hd ** -0.5)
    qi = jnp.arange(W)[:, None]
    kj = jnp.arange(2 * W)[None, :]
    delta = qi + W - kj
    key_sub = (jnp.arange(nb) * W - W)[:, None, None] + kj[None]
    valid = (delta >= 0) & (delta <= W) & (key_sub >= 0)
    bias = -slopes[:, None, None, None] * (delta * dilation).astype(jnp.float32)
    scores = jnp.where(valid, scores + bias, -jnp.inf)
    mx = jnp.max(scores, axis=-1, keepdims=True)
    p = jnp.exp(scores - mx)
    den = jnp.sum(p, axis=-1, keepdims=True)
    o = jnp.einsum('brhnqk,brhnkc->brhnqc', p, vw) / den
    lse = (mx + jnp.log(den))[..., 0]
    o = o.reshape(B, dilation, H, Lp, hd)[:, :, :, :L].transpose(0, 3, 1, 2, 4).reshape(B, S, H, hd)
    lse = lse.reshape(B, dilation, H, Lp)[:, :, :, :L].transpose(0, 3, 1, 2).reshape(B, S, H)
    return o, lse


def dilated_mixture(q, k, v):
    B, S = q.shape[:2]
    slopes = alibi_slopes(DSW_HEADS)
    outs, lses = [], []
    for g, (window, dilation) in enumerate(DSW_GROUPS):
        hs = slice(g * DSW_HEADS_PER_GROUP, (g + 1) * DSW_HEADS_PER_GROUP)
        o, lse = dilated_window_attention(q[:, :, hs], k[:, :, hs], v[:, :, hs], slopes[hs], window, dilation)
        outs.append(o)
        lses.append(lse)
    alpha = jax.nn.softmax(jnp.stack(lses, axis=0), axis=0)
    o = jnp.sum(alpha[..., None] * jnp.stack(outs, axis=0), axis=0)
    return o.reshape(B, S, OUT_A).astype(q.dtype)


def stick_breaking_attention(q, k, v):
    B, S, H, hd = q.shape
    nq = S // BLOCK
    scale = hd ** -0.5
    qb = q.reshape(B, nq, BLOCK, H, hd).transpose(1, 0, 3, 2, 4)
    kt = k.transpose(0, 2, 1, 3)
    vt = v.transpose(0, 2, 1, 3).astype(jnp.float32)
    key_pos = jnp.arange(S)

    def block(args):
        q_blk, i = args
        z = jnp.einsum('bhqc,bhkc->bhqk', q_blk, kt).astype(jnp.float32) * scale
        qpos = i * BLOCK + jnp.arange(BLOCK)
        causal = key_pos[None, :] < qpos[:, None]
        log_stay = jnp.where(causal, jax.nn.log_sigmoid(-z), 0.0)
        later = lax.cumsum(log_stay, axis=3, reverse=True) - log_stay
        attn = jnp.where(causal, jnp.exp(jax.nn.log_sigmoid(z) + later), 0.0)
        return jnp.einsum('bhqk,bhkc->bhqc', attn, vt)

    o = lax.map(block, (qb, jnp.arange(nq)))
    return o.transpose(1, 0, 3, 2, 4).reshape(B, S, H * hd).astype(q.dtype)


def causal_depthwise_conv(a, w, b):
    S = a.shape[1]
    ap = jnp.pad(a, ((0, 0), (CONV_WIDTH - 1, 0), (0, 0)))
    y = b
    for i in range(CONV_WIDTH):
        y = y + ap[:, i:i + S] * w[i]
    return y


def setup_inputs(seed: int = 0) -> dict:
    key = jax.random.key(seed)
    ks = jax.random.split(key, 12)
    f32 = jnp.float32
    x = jax.random.normal(ks[0], (BATCH, SEQ, D_MODEL), f32)
    norm1 = 1.0 + 0.05 * jax.random.normal(ks[1], (DEPTH, D_MODEL), f32)
    w_in = jax.random.normal(ks[2], (DEPTH, D_MODEL, N_IN), f32) * D_MODEL ** -0.5
    b_gate = 0.01 * jax.random.normal(ks[3], (DEPTH, 2 * D_MODEL), f32)
    w_br = jax.random.normal(ks[4], (DEPTH, OUT_A + OUT_B, D_MODEL), f32) * OUT_A ** -0.5
    w_o = jax.random.normal(ks[5], (DEPTH, D_MODEL, D_MODEL), f32) * D_MODEL ** -0.5
    norm2 = 1.0 + 0.05 * jax.random.normal(ks[6], (DEPTH, D_MODEL), f32)
    w_up = jax.random.normal(ks[7], (DEPTH, D_MODEL, 2 * D_FF), f32) * D_MODEL ** -0.5
    conv_w = jax.random.normal(ks[8], (DEPTH, CONV_WIDTH, D_FF), f32) * CONV_WIDTH ** -0.5
    conv_b = 0.01 * jax.random.normal(ks[9], (DEPTH, D_FF), f32)
    w_down = jax.random.normal(ks[10], (DEPTH, D_FF, D_MODEL), f32) * D_FF ** -0.5
    norm_f = 1.0 + 0.05 * jax.random.normal(ks[11], (D_MODEL,), f32)
    return {"x": x, "norm1": norm1, "w_in": w_in, "b_gate": b_gate, "w_br": w_br, "w_o": w_o,
            "norm2": norm2, "w_up": w_up, "conv_w": conv_w, "conv_b": conv_b, "w_down": w_down,
            "norm_f": norm_f}


def reference(x, norm1, w_in, b_gate, w_br, w_o, norm2, w_up, conv_w, conv_b, w_down, norm_f):
    B, S, _ = x.shape
    splits = [W_A, 2 * W_A, 3 * W_A, 3 * W_A + W_B, 3 * W_A + 2 * W_B, 3 * W_A + 3 * W_B]
    for l in range(DEPTH):
        h = rmsnorm(x, norm1[l])
        proj = h @ w_in[l]
        qa, ka, va, qb, kb, vb, gate_pre = jnp.split(proj, splits, axis=-1)
        hd4 = lambda t, n: t.reshape(B, S, n, HEAD_DIM)
        o_a = dilated_mixture(hd4(qa, DSW_HEADS), hd4(ka, DSW_HEADS), hd4(va, DSW_HEADS))
        o_b = stick_breaking_attention(hd4(qb, SB_HEADS), hd4(kb, SB_HEADS), hd4(vb, SB_HEADS))
        gates = jax.nn.sigmoid(gate_pre + b_gate[l])
        g_a, g_b = gates[..., :D_MODEL], gates[..., D_MODEL:]
        merged = g_a * (o_a @ w_br[l, :OUT_A]) + g_b * (o_b @ w_br[l, OUT_A:])
        x = x + merged @ w_o[l]
        h2 = rmsnorm(x, norm2[l])
        up = h2 @ w_up[l]
        a, v = up[..., :D_FF], up[..., D_FF:]
        a = causal_depthwise_conv(a, conv_w[l], conv_b[l])
        x = x + (jax.nn.gelu(a, approximate=False) * v) @ w_down[l]
    return rmsnorm(x, norm_f)
```

```python
from contextlib import ExitStack
import numpy as np
import ml_dtypes
import concourse.bass as bass
import concourse.mybir as mybir
from concourse.bass_utils import run_bass_kernel_spmd

F32 = mybir.dt.float32
BF16 = mybir.dt.bfloat16
AF = mybir.ActivationFunctionType
ALU = mybir.AluOpType
AX = mybir.AxisListType

D_MODEL = 1024
BATCH = 4
SEQ = 8192
DEPTH = 2
HD = 64
D_FF = 2816
RMS_EPS = 1e-6
NCORES = 8
TOK = SEQ // 2
NEG = -30000.0


class Em:
    def __init__(self, nc):
        self.nc = nc
        self.eng = {"pe": nc.tensor, "act": nc.scalar, "dve": nc.vector, "pool": nc.gpsimd, "sp": nc.sync}
        self.sems = {}
        self.cnt = {}
        self.waited = {e: {} for e in self.eng}
        self.last_w = {}
        self.readers = {}
        self.ekey = {}
        self.gen = 0
        for e in self.eng:
            self.ekey[e] = "E_" + e
            self._mksem("E_" + e)

    def _mksem(self, key):
        if key not in self.sems:
            self.sems[key] = self.nc.alloc_semaphore(name=key)
            self.cnt[key] = 0
        return self.sems[key]

    def _wait(self, e, tok):
        key, val = tok
        if self.waited[e].get(key, 0) >= val:
            return
        self.eng[e].wait_ge(self.sems[key], val)
        self.waited[e][key] = val

    def _deps(self, e, reads, writes, skip_same=False):
        own = self.ekey[e]
        toks = []
        for r in reads:
            if r in self.last_w:
                toks.append(self.last_w[r])
        for w in writes:
            if w in self.last_w:
                toks.append(self.last_w[w])
            toks.extend(self.readers.get(w, ()))
        for t in toks:
            if skip_same and t[0] == own:
                continue
            self._wait(e, t)

    def _record(self, tok, reads, writes):
        for r in reads:
            self.readers.setdefault(r, []).append(tok)
        for w in writes:
            self.last_w[w] = tok
            self.readers[w] = []

    def op(self, e, fn, reads=(), writes=(), sig=True):
        self._deps(e, reads, writes, skip_same=(e == "pe"))
        ins = fn()
        key = self.ekey[e]
        if sig:
            self.cnt[key] += 1
            ins.then_inc(self.sems[key], 1)
            tok = (key, self.cnt[key])
        else:
            tok = (key, self.cnt[key] + 1)
        self._record(tok, reads, writes)
        return ins

    def dma(self, q, pairs, semkey, reads=(), writes=()):
        self._mksem(semkey)
        self._deps(q, reads, writes)
        for (o, i) in pairs:
            self.cnt[semkey] += 16
            self.eng[q].dma_start(out=o, in_=i).then_inc(self.sems[semkey], 16)
        tok = (semkey, self.cnt[semkey])
        self._record(tok, reads, writes)
        return tok

    def barrier(self):
        for e in self.eng:
            for key in self.sems:
                if self.cnt[key] > 0:
                    self._wait(e, (key, self.cnt[key]))
        for e in self.eng:
            if self.cnt[self.ekey[e]] > 6000:
                self.gen += 1
                self.ekey[e] = f"E_{e}_{self.gen}"
                self._mksem(self.ekey[e])

    def finish(self, e="sp"):
        for tok in list(self.last_w.values()):
            self._wait(e, tok)
        for lst in self.readers.values():
            for tok in lst:
                self._wait(e, tok)


def _bf16(a):
    return np.asarray(a, dtype=np.float32).astype(ml_dtypes.bfloat16)


def emit_norm_transpose(em, nc, xs, blk_res, hb, hT, tp_ps, ident, ssq, rstd, nhalf, junk, tag, nblk, cpy_eng="act"):
    st = tag[0]
    for b in range(nblk):
        em.op("act", lambda b=b: nc.scalar.activation(out=junk[:, :], in_=xs[:, b, :], func=AF.Square,
                                                      accum_out=ssq[:, b:b + 1]),
              reads=[blk_res[b]], writes=[(st, "junk"), (st, "ssq", b)])
    em.op("pool", lambda: nc.gpsimd.tensor_scalar(out=rstd[:, 0:nblk], in0=ssq[:, 0:nblk], scalar1=1.0 / D_MODEL,
                                                  scalar2=RMS_EPS, op0=ALU.mult, op1=ALU.add),
          reads=[(st, "ssq", b) for b in range(nblk)], writes=[(st, "rstd")])
    em.op("pool", lambda: nc.gpsimd.tensor_tensor(out=rstd[:, 0:nblk], in0=rstd[:, 0:nblk], in1=nhalf[:, 0:nblk], op=ALU.pow),
          reads=[(st, "rstd"), "nhalf"], writes=[(st, "rstd")])
    for b in range(nblk):
        em.op("dve", lambda b=b: nc.vector.tensor_scalar(out=hb[:, b, :], in0=xs[:, b, :], scalar1=rstd[:, b:b + 1],
                                                         scalar2=None, op0=ALU.mult),
              reads=[blk_res[b], (st, "rstd")], writes=[(st, "hb", b)])
    for b in range(nblk):
        ps = tp_ps[b % len(tp_ps)]
        psr = (st, "tp", b % len(tp_ps))
        for kc in range(8):
            em.op("pe", lambda b=b, kc=kc, ps=ps: nc.tensor.transpose(out=ps[:, kc, :], in_=hb[:, b, kc * 128:(kc + 1) * 128],
                                                                       identity=ident[:, :]),
                  reads=[(st, "hb", b), "ident"], writes=[psr], sig=(kc == 7))
        if cpy_eng == "act":
            em.op("act", lambda b=b, ps=ps: nc.scalar.copy(out=hT[:, :, b * 128:(b + 1) * 128], in_=ps[:, :, :]),
                  reads=[psr], writes=[(tag, "hT", b)])
        else:
            em.op("dve", lambda b=b, ps=ps: nc.vector.tensor_copy(out=hT[:, :, b * 128:(b + 1) * 128], in_=ps[:, :, :]),
                  reads=[psr], writes=[(tag, "hT", b)])


CAST_CYCLE = ["dve", "act", "dve", "act", "pool"]


def emit_load_w(em, nc, q, w_dram, nk, pieces, Wb, gvec, stages, res, ctr=[0]):
    for kc in range(nk):
        for (c0, n, d0, sc) in pieces:
            i = ctr[0] % len(stages)
            st, sres, skey = stages[i]
            qq = q if (ctr[0] % 2 == 0) else "act"
            em.dma(qq, [(st[:, 0:n], w_dram[kc * 128:(kc + 1) * 128, c0:c0 + n])], skey, writes=sres)
            eng = CAST_CYCLE[ctr[0] % len(CAST_CYCLE)]
            ctr[0] += 1
            if eng == "act" and gvec is not None and float(sc) != 1.0:
                eng = "dve"
            if eng == "act":
                scale = float(sc) if gvec is None else gvec[:, kc:kc + 1]
                em.op("act", lambda st=st, n=n, d0=d0, kc=kc, scale=scale: nc.scalar.activation(
                    out=Wb[:, kc, d0:d0 + n], in_=st[:, 0:n], func=AF.Copy, scale=scale),
                    reads=list(sres) + (["gvec"] if gvec is not None else []), writes=[(res, kc, d0)])
                continue
            E = nc.vector if eng == "dve" else nc.gpsimd
            if gvec is None:
                em.op(eng, lambda E=E, st=st, n=n, d0=d0, kc=kc, sc=sc: E.tensor_scalar(
                    out=Wb[:, kc, d0:d0 + n], in0=st[:, 0:n], scalar1=float(sc), scalar2=None, op0=ALU.mult),
                    reads=sres, writes=[(res, kc, d0)])
            else:
                em.op(eng, lambda E=E, st=st, n=n, d0=d0, kc=kc, sc=sc: E.tensor_scalar(
                    out=Wb[:, kc, d0:d0 + n], in0=st[:, 0:n], scalar1=gvec[:, kc:kc + 1], scalar2=float(sc),
                    op0=ALU.mult, op1=ALU.mult),
                    reads=list(sres) + ["gvec"], writes=[(res, kc, d0)])


P1_COLMAP = [(0, 768, 0, 0.125), (768, 768, 768, 1.0), (2304, 256, 1536, 0.125), (2560, 256, 1792, 1.0),
             (1536, 768, 2048, 1.0), (2816, 256, 2816, 1.0)]


def build_p1(ntok=TOK):
    nc = bass.Bass("TRN2", target_bir_lowering=False)
    x_d = nc.dram_tensor("x", [ntok, D_MODEL], F32, kind="ExternalInput").ap()
    w_d = nc.dram_tensor("w", [D_MODEL, 3072], F32, kind="ExternalInput").ap()
    g_d = nc.dram_tensor("g", [128, 8], F32, kind="ExternalInput").ap()
    id_d = nc.dram_tensor("ident", [128, 128], BF16, kind="ExternalInput").ap()
    qk_d = nc.dram_tensor("qk", [16, 128, ntok], BF16, kind="ExternalOutput").ap()
    v_d = nc.dram_tensor("v", [ntok, 1024], BF16, kind="ExternalOutput").ap()
    em = Em(nc)
    emit_p1(nc, em, "p1", x_d, w_d, g_d, id_d, qk_d, v_d, ntok)
    em.finish("sp")
    return nc


def emit_p1(nc, em, pfx, x_d, w_d, g_d, id_d, qk_d, v_d, ntok):
    NT = ntok // 512
    with ExitStack() as stk:
        SB = lambda name, shape, dt: stk.enter_context(nc.sbuf_tensor(pfx + "s_" + name, shape, dt))
        PS = lambda name, shape, dt: stk.enter_context(nc.psum_tensor(pfx + "p_" + name, shape, dt))
        Wb = SB("Wb", [128, 8, 3072], BF16)
        wst0 = SB("wst0", [128, 3072], F32)
        wst1 = SB("wst1", [128, 3072], F32)
        xs0 = SB("xs0", [128, 4, 1024], F32)
        xs1 = SB("xs1", [128, 4, 1024], F32)
        hb = SB("hb", [128, 4, 1024], BF16)
        hT0 = SB("hT0", [128, 8, 512], BF16)
        hT1 = SB("hT1", [128, 8, 512], BF16)
        junk = SB("junk", [128, 1024], BF16)
        ssq = SB("ssq", [128, 4], F32)
        rstd = SB("rstd", [128, 4], F32)
        nhalf = SB("nhalf", [128, 4], F32)
        gv = SB("gv", [128, 8], F32)
        ident = SB("ident", [128, 128], BF16)
        qst0 = SB("qst0", [128, 512], BF16)
        qst1 = SB("qst1", [128, 512], BF16)
        qst2 = SB("qst2", [128, 512], BF16)
        qst3 = SB("qst3", [128, 512], BF16)
        vst0 = SB("vst0", [128, 4, 1024], BF16)
        vst1 = SB("vst1", [128, 4, 1024], BF16)
        tp0 = PS("tp0", [128, 8, 128], BF16)
        tp1 = PS("tp1", [128, 8, 128], BF16)
        mm0 = PS("mm0", [128, 512], F32)
        mm1 = PS("mm1", [128, 512], F32)
        mm2 = PS("mm2", [128, 512], F32)
        mm3 = PS("mm3", [128, 512], F32)
        xs = [xs0, xs1]
        hTs = [hT0, hT1]
        qst = [qst0, qst1, qst2, qst3]
        vst = [vst0, vst1]
        mm = [mm0, mm1, mm2, mm3]
        em.op("pool", lambda: nc.gpsimd.memset(nhalf[:, :], -0.5), writes=["nhalf"])
        em.dma("sp", [(gv[:, :], g_d[:, :])], "D_gv", writes=["gvec"])
        em.dma("sp", [(ident[:, :], id_d[:, :])], "D_id", writes=["ident"])

        def load_x(ti):
            s = ti % 2
            src = x_d[ti * 512:(ti + 1) * 512, :].rearrange("(b p) f -> p b f", p=128)
            em.dma("sp", [(xs[s][:, b, :], src[:, b, :]) for b in range(4)], f"D_xs{s}",
                   writes=[("xs", s, b) for b in range(4)])

        load_x(0)
        emit_load_w(em, nc, "sp", w_d, 8, P1_COLMAP, Wb, gv,
                    [(wst0, [("wst", 0)], "D_wst0"), (wst1, [("wst", 1)], "D_wst1")], "Wb")
        wres = [("Wb", kc, d0) for kc in range(8) for (_, _, d0, _) in P1_COLMAP]
        mmi = 0
        qi = 0
        for ti in range(NT):
            s = ti % 2
            if ti + 1 < NT:
                load_x(ti + 1)
            hT = hTs[s]
            tag = ("n", s)
            emit_norm_transpose(em, nc, xs[s], [("xs", s, b) for b in range(4)], hb, hT, [tp0, tp1], ident, ssq, rstd,
                                nhalf, junk, tag, 4)
            hres = [(tag, "hT", b) for b in range(4)]
            for ch in range(16):
                ps = mm[mmi % 4]
                pr = ("mm", mmi % 4)
                mmi += 1
                for kc in range(8):
                    em.op("pe", lambda ps=ps, kc=kc, ch=ch: nc.tensor.matmul(
                        ps[:, :], lhsT=Wb[:, kc, ch * 128:(ch + 1) * 128], rhs=hT[:, kc, :], start=(kc == 0), stop=(kc == 7)),
                        reads=wres + hres if kc == 0 else (), writes=[pr], sig=(kc == 7))
                st = qst[qi % 4]
                sr = ("qst", qi % 4)
                qi += 1
                if ch % 2 == 0:
                    em.op("act", lambda st=st, ps=ps: nc.scalar.copy(out=st[:, :], in_=ps[:, :]), reads=[pr], writes=[sr])
                else:
                    em.op("dve", lambda st=st, ps=ps: nc.vector.tensor_copy(out=st[:, :], in_=ps[:, :]), reads=[pr], writes=[sr])
                em.dma("pool", [(qk_d[ch, :, ti * 512:(ti + 1) * 512], st[:, :])], f"D_qst{(qi - 1) % 4}",
                       reads=[sr], writes=[("qk_out", ch, ti)])
            vs = vst[s]
            for b in range(4):
                for half in range(2):
                    ps = mm[mmi % 4]
                    pr = ("mm", mmi % 4)
                    mmi += 1
                    for kc in range(8):
                        em.op("pe", lambda ps=ps, kc=kc, b=b, half=half: nc.tensor.matmul(
                            ps[:, :], lhsT=hT[:, kc, b * 128:(b + 1) * 128], rhs=Wb[:, kc, 2048 + half * 512:2048 + (half + 1) * 512],
                            start=(kc == 0), stop=(kc == 7)),
                            reads=wres + hres if kc == 0 else (), writes=[pr], sig=(kc == 7))
                    if half == 0:
                        em.op("act", lambda vs=vs, ps=ps, b=b: nc.scalar.copy(out=vs[:, b, 0:512], in_=ps[:, :]),
                              reads=[pr], writes=[("vst", s, b, 0)])
                    else:
                        em.op("dve", lambda vs=vs, ps=ps, b=b: nc.vector.tensor_copy(out=vs[:, b, 512:1024], in_=ps[:, :]),
                              reads=[pr], writes=[("vst", s, b, 1)])
            dst = v_d[ti * 512:(ti + 1) * 512, :].rearrange("(b p) f -> p b f", p=128)
            em.dma("pool", [(dst[:, b, :], vs[:, b, :]) for b in range(4)], f"D_vst{s}",
                   reads=[("vst", s, b, h) for b in range(4) for h in range(2)], writes=[("v_out", ti)])
        em.barrier()


NB3 = 3
TOKP = TOK + 128


def build_p3(final, ntok=TOKP, stage=2):
    nc = bass.Bass("TRN2", target_bir_lowering=False)
    x_d = nc.dram_tensor("x", [ntok, D_MODEL], F32, kind="ExternalInput").ap()
    o_d = nc.dram_tensor("oT", [512, ntok], BF16, kind="ExternalInput").ap()
    wg_d = nc.dram_tensor("wg", [D_MODEL, 2048], F32, kind="ExternalInput").ap()
    bg_d = nc.dram_tensor("bg", [128, 16], F32, kind="ExternalInput").ap()
    g1_d = nc.dram_tensor("g1", [128, 8], F32, kind="ExternalInput").ap()
    wbr_d = nc.dram_tensor("wbr", [512, D_MODEL], F32, kind="ExternalInput").ap()
    wo_d = nc.dram_tensor("wo", [D_MODEL, D_MODEL], F32, kind="ExternalInput").ap()
    g2_d = nc.dram_tensor("g2", [128, 8], F32, kind="ExternalInput").ap()
    wup_d = nc.dram_tensor("wup", [D_MODEL, 2 * D_FF], F32, kind="ExternalInput").ap()
    cw_d = nc.dram_tensor("cw", [128, 22, 3], F32, kind="ExternalInput").ap()
    cb_d = nc.dram_tensor("cb", [128, 22], F32, kind="ExternalInput").ap()
    wdn_d = nc.dram_tensor("wdn", [D_FF, D_MODEL], F32, kind="ExternalInput").ap()
    gf_d = nc.dram_tensor("gf", [128, D_MODEL], F32, kind="ExternalInput").ap()
    id_d = nc.dram_tensor("ident", [128, 128], BF16, kind="ExternalInput").ap()
    xo_d = nc.dram_tensor("xo", [ntok, D_MODEL], F32, kind="ExternalOutput").ap()
    xm_d = nc.dram_tensor("xmid", [ntok, D_MODEL], F32).ap()
    em = Em(nc)
    emit_p3(nc, em, "p3", final, x_d, o_d, wg_d, bg_d, g1_d, wbr_d, wo_d, g2_d, wup_d, cw_d, cb_d, wdn_d, gf_d, id_d, xo_d, xm_d, ntok, stage)
    em.finish("sp")
    return nc


def emit_p3(nc, em, pfx, final, x_d, o_d, wg_d, bg_d, g1_d, wbr_d, wo_d, g2_d, wup_d, cw_d, cb_d, wdn_d, gf_d, id_d, xo_d, xm_d, ntok, stage=2):
    NTK = NB3 * 128
    NT = ntok // NTK
    assert NT * NTK == ntok

    def tile_view(d, ti):
        return d[ti * NTK:(ti + 1) * NTK, :].rearrange("(b p) f -> p b f", p=128)

    with ExitStack() as stk:
        SB = lambda name, shape, dt: stk.enter_context(nc.sbuf_tensor(pfx + "a_" + name, shape, dt))
        PS = lambda name, shape, dt: stk.enter_context(nc.psum_tensor(pfx + "pa_" + name, shape, dt))
        Wg = SB("Wg", [128, 8, 2048], BF16)
        Wbr = SB("Wbr", [128, 4, 1024], BF16)
        Wo = SB("Wo", [128, 8, 1024], BF16)
        wst = [SB(f"wst{i}", [128, 2048], F32) for i in range(2)]
        xt = [SB(f"xt{i}", [128, NB3, 1024], F32) for i in range(2)]
        ot = [SB(f"ot{i}", [128, 4, NTK], BF16) for i in range(2)]
        hb = SB("hb", [128, NB3, 1024], BF16)
        hTs = [SB(f"hT{i}", [128, 8, NTK], BF16) for i in range(2)]
        gate = SB("gate", [128, 16, NTK], F32)
        mg = SB("mg", [128, 8, NTK], BF16)
        t1 = [SB(f"t1_{i}", [128, NTK], F32) for i in range(2)]
        t2 = [SB(f"t2_{i}", [128, NTK], F32) for i in range(2)]
        junk = SB("junk", [128, 1024], BF16)
        ssq = SB("ssq", [128, 4], F32)
        rstd = SB("rstd", [128, 4], F32)
        nhalf = SB("nhalf", [128, 4], F32)
        gv = SB("gv", [128, 8], F32)
        bg = SB("bg", [128, 16], F32)
        ident = SB("ident", [128, 128], BF16)
        tp = [PS(f"tp{i}", [128, 8, 128], BF16) for i in range(2)]
        mm = [PS(f"mm{i}", [128, 512], F32) for i in range(6)]
        stages = [(wst[i], [("wst", i)], f"D_awst{i}") for i in range(2)]

        em.op("pool", lambda: nc.gpsimd.memset(nhalf[:, :], -0.5), writes=["nhalf"])
        em.dma("sp", [(gv[:, :], g1_d[:, :])], "D_gv", writes=["gvec"])
        em.dma("sp", [(bg[:, :], bg_d[:, :])], "D_bg", writes=["bg"])
        em.dma("sp", [(ident[:, :], id_d[:, :])], "D_id", writes=["ident"])

        def load_a(ti):
            s = ti % 2
            src = tile_view(x_d, ti)
            em.dma("sp", [(xt[s][:, b, :], src[:, b, :]) for b in range(NB3)], f"D_axt{s}",
                   writes=[("xt", s, b) for b in range(NB3)])
            osrc = o_d[:, ti * NTK:(ti + 1) * NTK].rearrange("(c p) t -> p c t", p=128)
            em.dma("sp", [(ot[s][:, c, :], osrc[:, c, :]) for c in range(4)], f"D_aot{s}", writes=[("ot", s)])

        load_a(0)
        emit_load_w(em, nc, "sp", wg_d, 8, [(0, 2048, 0, 1.0)], Wg, gv, stages, "Wg")
        emit_load_w(em, nc, "sp", wbr_d, 4, [(0, 1024, 0, 1.0)], Wbr, None, stages, "Wbr")
        emit_load_w(em, nc, "sp", wo_d, 8, [(0, 1024, 0, 1.0)], Wo, None, stages, "Wo")
        wg_res = [("Wg", kc, 0) for kc in range(8)]
        wbr_res = [("Wbr", kc, 0) for kc in range(4)]
        wo_res = [("Wo", kc, 0) for kc in range(8)]
        mmi = [0]

        def next_mm():
            i = mmi[0] % len(mm)
            mmi[0] += 1
            return mm[i], ("mm", i)

        def norm_a(ti):
            s = ti % 2
            emit_norm_transpose(em, nc, xt[s], [("xt", s, b) for b in range(NB3)], hb, hTs[s], tp, ident, ssq, rstd, nhalf,
                                junk, ("na", s), NB3)

        norm_a(0)
        for ti in range(NT):
            s = ti % 2
            if ti + 1 < NT:
                load_a(ti + 1)
            hT = hTs[s]
            tag = ("na", s)
            xres = [("xt", s, b) for b in range(NB3)]
            hres = [(tag, "hT", b) for b in range(NB3)]
            for gc in range(16):
                ps, pr = next_mm()
                for kc in range(8):
                    em.op("pe", lambda ps=ps, kc=kc, gc=gc: nc.tensor.matmul(
                        ps[:, 0:NTK], lhsT=Wg[:, kc, gc * 128:(gc + 1) * 128], rhs=hT[:, kc, :], start=(kc == 0), stop=(kc == 7)),
                        reads=(wg_res + hres) if kc == 0 else (), writes=[pr], sig=(kc == 7))
                em.op("act", lambda ps=ps, gc=gc: nc.scalar.activation(out=gate[:, gc, :], in_=ps[:, 0:NTK], func=AF.Sigmoid,
                                                                      bias=bg[:, gc:gc + 1], scale=1.0),
                      reads=[pr, "bg"], writes=[("gate", gc)])
            for n in range(8):
                psA, prA = next_mm()
                for kc in range(2):
                    em.op("pe", lambda psA=psA, kc=kc, n=n: nc.tensor.matmul(
                        psA[:, 0:NTK], lhsT=Wbr[:, kc, n * 128:(n + 1) * 128], rhs=ot[s][:, kc, :], start=(kc == 0), stop=(kc == 1)),
                        reads=(wbr_res + [("ot", s)]) if kc == 0 else (), writes=[prA], sig=(kc == 1))
                psB, prB = next_mm()
                for kc in range(2, 4):
                    em.op("pe", lambda psB=psB, kc=kc, n=n: nc.tensor.matmul(
                        psB[:, 0:NTK], lhsT=Wbr[:, kc, n * 128:(n + 1) * 128], rhs=ot[s][:, kc, :], start=(kc == 2), stop=(kc == 3)),
                        reads=(wbr_res + [("ot", s)]) if kc == 2 else (), writes=[prB], sig=(kc == 3))
                j = n % 2
                em.op("dve", lambda psA=psA, n=n, j=j: nc.vector.tensor_tensor(out=t1[j][:, :], in0=gate[:, n, :], in1=psA[:, 0:NTK], op=ALU.mult),
                      reads=[prA, ("gate", n)], writes=[("t1", j)])
                em.op("dve", lambda psB=psB, n=n, j=j: nc.vector.tensor_tensor(out=t2[j][:, :], in0=gate[:, 8 + n, :], in1=psB[:, 0:NTK], op=ALU.mult),
                      reads=[prB, ("gate", 8 + n)], writes=[("t2", j)])
                em.op("pool", lambda n=n, j=j: nc.gpsimd.tensor_tensor(out=mg[:, n, :], in0=t1[j][:, :], in1=t2[j][:, :], op=ALU.add),
                      reads=[("t1", j), ("t2", j)], writes=[("mg", n)])
            if ti + 1 < NT:
                norm_a(ti + 1)
            mres = [("mg", n) for n in range(8)]
            for b in range(NB3):
                for half in range(2):
                    ps, pr = next_mm()
                    for kc in range(8):
                        em.op("pe", lambda ps=ps, kc=kc, b=b, half=half: nc.tensor.matmul(
                            ps[:, :], lhsT=mg[:, kc, b * 128:(b + 1) * 128], rhs=Wo[:, kc, half * 512:(half + 1) * 512],
                            start=(kc == 0), stop=(kc == 7)),
                            reads=(wo_res + mres) if kc == 0 else (), writes=[pr], sig=(kc == 7))
                    em.op("dve", lambda ps=ps, b=b, half=half: nc.vector.tensor_tensor(
                        out=xt[s][:, b, half * 512:(half + 1) * 512], in0=xt[s][:, b, half * 512:(half + 1) * 512], in1=ps[:, :], op=ALU.add),
                        reads=[pr, ("xt", s, b)], writes=[("xt", s, b)])
            dst = tile_view(xm_d if stage == 2 else xo_d, ti)
            em.dma("pool", [(dst[:, b, :], xt[s][:, b, :]) for b in range(NB3)], f"D_axo{s}",
                   reads=xres, writes=[("xmid", ti)])
        em.barrier()
    if stage == 1:
        return

    em2 = em
    with ExitStack() as stk:
        SB = lambda name, shape, dt: stk.enter_context(nc.sbuf_tensor(pfx + "b_" + name, shape, dt))
        PS = lambda name, shape, dt: stk.enter_context(nc.psum_tensor(pfx + "pb_" + name, shape, dt))
        Wup = SB("Wup", [128, 8, 2 * D_FF], BF16)
        Wdn = SB("Wdn", [128, 22, 1024], BF16)
        xt = [SB(f"xt{i}", [128, NB3, 1024], F32) for i in range(2)]
        hb = SB("hb", [128, NB3, 1024], BF16)
        hTs = [SB(f"hT{i}", [128, 8, NTK], BF16) for i in range(2)]
        yT = SB("yT", [128, 22, NTK], BF16)
        asb = [SB(f"asb{i}", [128, NTK + 2], F32) for i in range(2)]
        cc = [SB(f"cc{i}", [128, NTK], F32) for i in range(2)]
        gl = [SB(f"gl{i}", [128, NTK], F32) for i in range(2)]
        hist = SB("hist", [128, 22, 2], F32)
        junk = SB("junk", [128, 1024], BF16)
        ssq = SB("ssq", [128, 4], F32)
        rstd = SB("rstd", [128, 4], F32)
        nhalf = SB("nhalf", [128, 4], F32)
        gv = SB("gv", [128, 8], F32)
        cw = SB("cw", [128, 22, 3], F32)
        cb = SB("cb", [128, 22], F32)
        ident = SB("ident", [128, 128], BF16)
        gfb = SB("gfb", [128, 1024], F32)
        tp = [PS(f"tp{i}", [128, 8, 128], BF16) for i in range(2)]
        mm = [PS(f"mm{i}", [128, 512], F32) for i in range(6)]
        stages = [(xt[i][:, :, :].rearrange("p b f -> p (b f)"), [("xt", i, b) for b in range(NB3)], f"D_bxt{i}") for i in range(2)]

        em.op("pool", lambda: nc.gpsimd.memset(nhalf[:, :], -0.5), writes=["nhalf"])
        em.op("pool", lambda: nc.gpsimd.memset(hist[:, :, :], 0.0), writes=[("hist", fc) for fc in range(22)])
        em.dma("sp", [(gv[:, :], g2_d[:, :])], "D_gv", writes=["gvec"])
        em.dma("sp", [(cw[:, :, :], cw_d[:, :, :])], "D_cw", writes=["cw"])
        em.dma("sp", [(cb[:, :], cb_d[:, :])], "D_cb", writes=["cb"])
        em.dma("sp", [(ident[:, :], id_d[:, :])], "D_id", writes=["ident"])
        em.dma("sp", [(gfb[:, :], gf_d[:, :])], "D_gf", writes=["gfb"])
        emit_load_w(em, nc, "sp", wup_d, 8, [(0, 2816, 0, 1.0), (2816, 2816, 2816, 1.0)], Wup, gv, stages, "Wup")
        emit_load_w(em, nc, "sp", wdn_d, 22, [(0, 1024, 0, 1.0)], Wdn, None, stages, "Wdn")
        wup_res = [("Wup", kc, d0) for kc in range(8) for d0 in (0, 2816)]
        wdn_res = [("Wdn", kc, 0) for kc in range(22)]
        mmi = [0]

        def next_mm():
            i = mmi[0] % len(mm)
            mmi[0] += 1
            return mm[i], ("mm", i)

        def load_b(ti):
            s = ti % 2
            src = tile_view(xm_d, ti)
            em.dma("sp", [(xt[s][:, b, :], src[:, b, :]) for b in range(NB3)], f"D_bxt{s}",
                   reads=[("xmid", ti)], writes=[("xt", s, b) for b in range(NB3)])

        def norm_b(ti):
            s = ti % 2
            emit_norm_transpose(em, nc, xt[s], [("xt", s, b) for b in range(NB3)], hb, hTs[s], tp, ident, ssq, rstd, nhalf,
                                junk, ("nb", s), NB3)

        load_b(0)
        norm_b(0)
        for ti in range(NT):
            s = ti % 2
            if ti + 1 < NT:
                load_b(ti + 1)
            hT = hTs[s]
            tag = ("nb", s)
            xres = [("xt", s, b) for b in range(NB3)]
            hres = [(tag, "hT", b) for b in range(NB3)]
            for fc in range(22):
                j = fc % 2
                psa, pra = next_mm()
                for kc in range(8):
                    em.op("pe", lambda psa=psa, kc=kc, fc=fc: nc.tensor.matmul(
                        psa[:, 0:NTK], lhsT=Wup[:, kc, fc * 128:(fc + 1) * 128], rhs=hT[:, kc, :], start=(kc == 0), stop=(kc == 7)),
                        reads=(wup_res + hres) if kc == 0 else (), writes=[pra], sig=(kc == 7))
                psv, prv = next_mm()
                for kc in range(8):
                    em.op("pe", lambda psv=psv, kc=kc, fc=fc: nc.tensor.matmul(
                        psv[:, 0:NTK], lhsT=Wup[:, kc, D_FF + fc * 128:D_FF + (fc + 1) * 128], rhs=hT[:, kc, :], start=(kc == 0), stop=(kc == 7)),
                        reads=(wup_res + hres) if kc == 0 else (), writes=[prv], sig=(kc == 7))
                A = asb[j]
                ar = ("asb", j)
                em.op("pool", lambda A=A, fc=fc: nc.gpsimd.tensor_copy(out=A[:, 0:2], in_=hist[:, fc, :]),
                      reads=[("hist", fc)], writes=[(ar, "h")])
                em.op("act", lambda A=A, psa=psa: nc.scalar.copy(out=A[:, 2:NTK + 2], in_=psa[:, 0:NTK]),
                      reads=[pra], writes=[(ar, "m")])
                em.op("pool", lambda A=A, fc=fc: nc.gpsimd.tensor_copy(out=hist[:, fc, :], in_=A[:, NTK:NTK + 2]),
                      reads=[(ar, "m"), (ar, "h")], writes=[("hist", fc)])
                C = cc[j]
                cr = ("cc", j)
                em.op("pool", lambda A=A, C=C, fc=fc: nc.gpsimd.tensor_scalar(
                    out=C[:, :], in0=A[:, 2:NTK + 2], scalar1=cw[:, fc, 2:3], scalar2=cb[:, fc:fc + 1], op0=ALU.mult, op1=ALU.add),
                    reads=[(ar, "m"), "cw", "cb"], writes=[cr])
                em.op("dve", lambda A=A, C=C, fc=fc: nc.vector.scalar_tensor_tensor(
                    out=C[:, :], in0=A[:, 1:NTK + 1], scalar=cw[:, fc, 1:2], in1=C[:, :], op0=ALU.mult, op1=ALU.add),
                    reads=[(ar, "m"), (ar, "h"), "cw", cr], writes=[cr])
                em.op("dve", lambda A=A, C=C, fc=fc: nc.vector.scalar_tensor_tensor(
                    out=C[:, :], in0=A[:, 0:NTK], scalar=cw[:, fc, 0:1], in1=C[:, :], op0=ALU.mult, op1=ALU.add),
                    reads=[(ar, "m"), (ar, "h"), "cw", cr], writes=[cr])
                G = gl[j]
                gr = ("gl", j)
                em.op("act", lambda G=G, C=C: nc.scalar.activation(out=G[:, :], in_=C[:, :], func=AF.Erf, scale=0.7071067811865476),
                      reads=[cr], writes=[gr])
                em.op("dve", lambda G=G, C=C: nc.vector.scalar_tensor_tensor(
                    out=G[:, :], in0=G[:, :], scalar=1.0, in1=C[:, :], op0=ALU.add, op1=ALU.mult),
                    reads=[gr, cr], writes=[gr])
                em.op("dve", lambda G=G, psv=psv, fc=fc: nc.vector.scalar_tensor_tensor(
                    out=yT[:, fc, :], in0=G[:, :], scalar=0.5, in1=psv[:, 0:NTK], op0=ALU.mult, op1=ALU.mult),
                    reads=[gr, prv], writes=[("yT", fc)])
            yres = [("yT", fc) for fc in range(22)]
            if ti + 1 < NT:
                norm_b(ti + 1)
            for b in range(NB3):
                for half in range(2):
                    ps, pr = next_mm()
                    for fc in range(22):
                        em.op("pe", lambda ps=ps, fc=fc, b=b, half=half: nc.tensor.matmul(
                            ps[:, :], lhsT=yT[:, fc, b * 128:(b + 1) * 128], rhs=Wdn[:, fc, half * 512:(half + 1) * 512],
                            start=(fc == 0), stop=(fc == 21)),
                            reads=(wdn_res + yres) if fc == 0 else (), writes=[pr], sig=(fc == 21))
                    em.op("dve", lambda ps=ps, b=b, half=half: nc.vector.tensor_tensor(
                        out=xt[s][:, b, half * 512:(half + 1) * 512], in0=xt[s][:, b, half * 512:(half + 1) * 512], in1=ps[:, :], op=ALU.add),
                        reads=[pr, ("xt", s, b)], writes=[("xt", s, b)])
            if final:
                ftag = ("nf", s)
                for b in range(NB3):
                    em.op("act", lambda b=b: nc.scalar.activation(out=junk[:, :], in_=xt[s][:, b, :], func=AF.Square,
                                                                  accum_out=ssq[:, b:b + 1]),
                          reads=[("xt", s, b)], writes=[(tag[0], "junk"), (tag[0], "ssq", b)])
                em.op("pool", lambda: nc.gpsimd.tensor_scalar(out=rstd[:, 0:NB3], in0=ssq[:, 0:NB3], scalar1=1.0 / D_MODEL,
                                                              scalar2=RMS_EPS, op0=ALU.mult, op1=ALU.add),
                      reads=[(tag[0], "ssq", b) for b in range(NB3)], writes=[(tag[0], "rstd")])
                em.op("pool", lambda: nc.gpsimd.tensor_tensor(out=rstd[:, 0:NB3], in0=rstd[:, 0:NB3], in1=nhalf[:, 0:NB3], op=ALU.pow),
                      reads=[(tag[0], "rstd"), "nhalf"], writes=[(tag[0], "rstd")])
                for b in range(NB3):
                    em.op("dve", lambda b=b: nc.vector.scalar_tensor_tensor(
                        out=xt[s][:, b, :], in0=xt[s][:, b, :], scalar=rstd[:, b:b + 1], in1=gfb[:, :], op0=ALU.mult, op1=ALU.mult),
                        reads=[("xt", s, b), (tag[0], "rstd"), "gfb"], writes=[("xt", s, b)])
            dst = tile_view(xo_d, ti)
            em.dma("pool", [(dst[:, b, :], xt[s][:, b, :]) for b in range(NB3)], f"D_bxo{s}",
                   reads=xres, writes=[("xo", ti)])
        em.barrier()


DSW = ((128, 1), (512, 4), (2048, 16))
CH = 2048


def p2_consts(p):
    k = np.arange(128)[:, None]
    q = np.arange(128)[None, :]
    slopes = 2.0 ** (-8.0 * np.arange(1, 13) / 12)
    bias = np.zeros((128, 6, 512), np.float32)
    for g, (w, d) in enumerate(DSW):
        for j in range(2):
            sl = slopes[g * 4 + 2 * p + j]
            prev = np.where(k >= q, -sl * d * (q + 128 - k), NEG)
            cur = np.where(k <= q, -sl * d * (q - k), NEG)
            bias[:, g * 2 + j, :] = np.concatenate([prev, cur, prev, cur], axis=1)
    mtri = (q > k).astype(np.float32)
    ltri = -(k >= q).astype(np.float32)
    return {"biasA": bias, "mtri": _bf16(mtri), "ltri": _bf16(ltri), "nones": _bf16(-np.ones((128, 128)))}


def build_p2(S=SEQ):
    nc = bass.Bass("TRN2", target_bir_lowering=False)
    qkA_d = nc.dram_tensor("qkA", [3, 2, 128, S], BF16, kind="ExternalInput").ap()
    qkB_d = nc.dram_tensor("qkB", [2, 128, S], BF16, kind="ExternalInput").ap()
    vA_d = nc.dram_tensor("vA", [3, S, 128], BF16, kind="ExternalInput").ap()
    vB_d = nc.dram_tensor("vB", [S, 128], BF16, kind="ExternalInput").ap()
    bias_d = nc.dram_tensor("biasA", [128, 6, 512], F32, kind="ExternalInput").ap()
    mtri_d = nc.dram_tensor("mtri", [128, 128], BF16, kind="ExternalInput").ap()
    ltri_d = nc.dram_tensor("ltri", [128, 128], BF16, kind="ExternalInput").ap()
    nones_d = nc.dram_tensor("nones", [128, 128], BF16, kind="ExternalInput").ap()
    o_d = nc.dram_tensor("oT", [256, S], BF16, kind="ExternalOutput").ap()
    den_d = nc.dram_tensor("den_scr", [2, S], F32).ap()
    em = Em(nc)
    emit_p2(nc, em, "p2", lambda g: qkA_d[g, 0, :, :], lambda g: qkA_d[g, 1, :, :], lambda g: vA_d[g, :, :],
            qkB_d[0, :, :], qkB_d[1, :, :], vB_d, bias_d, mtri_d, ltri_d, nones_d, o_d, 0, 128, den_d, S)
    em.finish("sp")
    return nc


def emit_p2(nc, em, pfx, qA, kA, vA, qB_d, kB_d, vB_d, bias_d, mtri_d, ltri_d, nones_d, o_d, oa_row0, ob_row0, den_d, S):
    NQT = S // 512
    NKB = S // 128
    with ExitStack() as stk:
        SB = lambda name, shape, dt: stk.enter_context(nc.sbuf_tensor(pfx + "s_" + name, shape, dt))
        PS = lambda name, shape, dt: stk.enter_context(nc.psum_tensor(pfx + "p_" + name, shape, dt))
        QB = SB("QB", [128, S], BF16)
        KB = SB("KB", [128, S], BF16)
        VB = SB("VB", [128, NKB, 128], BF16)
        mtri = SB("mtri", [128, 128], BF16)
        ltri = SB("ltri", [128, 128], BF16)
        nones = SB("nones", [128, 128], BF16)
        Et = [SB(f"E{i}", [128, 512], F32) for i in range(3)]
        SPt = [SB(f"SP{i}", [128, 512], BF16) for i in range(4)]
        PBt = [SB(f"PB{i}", [128, 512], BF16) for i in range(3)]
        Spre = [SB(f"Spre{i}", [128, 512], BF16) for i in range(2)]
        obst = [SB(f"obst{i}", [64, 512], BF16) for i in range(2)]
        zb = [PS(f"z{i}", [128, 512], F32) for i in range(4)]
        po = [PS(f"po{i}", [64, 512], F32) for i in range(2)]
        QA = [SB(f"QA{i}", [128, CH], BF16) for i in range(2)]
        KA = [SB(f"KA{i}", [128, 2 * CH], BF16) for i in range(2)]
        VA = [SB(f"VA{i}", [128, 32, 2, 65], BF16) for i in range(2)]
        biasA = SB("biasA", [128, 6, 512], F32)
        TA = [SB(f"TA{i}", [128, 512], F32) for i in range(2)]
        PA = [SB(f"PA{i}", [128, 512], BF16) for i in range(2)]
        acc = [SB(f"acc{i}", [65, CH], F32) for i in range(2)]
        dbc = [SB(f"dbc{i}", [64, CH], F32) for i in range(2)]
        oast = [SB(f"oast{i}", [64, CH], BF16) for i in range(2)]
        sA = [zb[0], zb[1]]
        oA = [zb[2], zb[3]]

        em.dma("sp", [(mtri[:, :], mtri_d[:, :])], "D_c0", writes=["mtri"])
        em.dma("sp", [(ltri[:, :], ltri_d[:, :])], "D_c1", writes=["ltri"])
        em.dma("sp", [(nones[:, :], nones_d[:, :])], "D_c2", writes=["nones"])
        em.dma("sp", [(biasA[:, :, :], bias_d[:, :, :])], "D_c3", writes=["biasA"])
        em.dma("sp", [(QB[:, :], qB_d)], "D_QB", writes=["QB"])
        em.dma("sp", [(KB[:, :], kB_d)], "D_KB", writes=["KB"])
        em.dma("sp", [(VB[:, :, :], vB_d.rearrange("(n p) c -> p n c", p=128))], "D_VB", writes=["VB"])
        for i in range(2):
            em.op("pool", lambda i=i: nc.gpsimd.memset(VA[i][:, :, :, 64:65], 1.0), writes=[("VA", i)])

        steps = []
        per_head = [[(h, i, kb) for i in range(NQT) for kb in range(4 * i + 3, -1, -1)] for h in range(2)]
        for a, b in zip(*per_head):
            steps += [a, b]

        def geom(step):
            h, i, kb = step
            j = kb - 4 * i
            c0 = 128 * j if j >= 0 else 0
            return h, i, kb, j, c0

        NZ, NE, NSP, NPB = len(zb), len(Et), len(SPt), len(PBt)

        def s0_z(n):
            h, i, kb, j, c0 = geom(steps[n])
            z, zr = zb[n % NZ], ("z", n % NZ)
            hp = slice(h * 64, (h + 1) * 64)
            em.op("pe", lambda: nc.tensor.matmul(z[:, c0:512], lhsT=KB[hp, kb * 128:(kb + 1) * 128],
                                                 rhs=QB[hp, i * 512 + c0:(i + 1) * 512], start=True, stop=True),
                  reads=["QB", "KB"], writes=[zr])

        def s1_exp(n):
            h, i, kb, j, c0 = geom(steps[n])
            z, zr = zb[n % NZ], ("z", n % NZ)
            E, er = Et[n % NE], ("E", n % NE)
            em.op("act", lambda: nc.scalar.activation(out=E[:, c0:512], in_=z[:, c0:512], func=AF.Exp), reads=[zr], writes=[er])

        def s2_ln(n):
            h, i, kb, j, c0 = geom(steps[n])
            E, er = Et[n % NE], ("E", n % NE)
            SPn, sr = SPt[n % NSP], ("SP", n % NSP)
            em.op("act", lambda: nc.scalar.activation(out=SPn[:, c0:512], in_=E[:, c0:512], func=AF.Ln, bias=1.0, scale=1.0),
                  reads=[er], writes=[sr])
            if j >= 0:
                em.op("dve", lambda: nc.vector.tensor_tensor(out=SPn[:, c0:c0 + 128], in0=SPn[:, c0:c0 + 128], in1=mtri[:, :], op=ALU.mult),
                      reads=[sr, "mtri"], writes=[sr])

        def s3_cum(n):
            h, i, kb, j, c0 = geom(steps[n])
            z, zr = zb[n % NZ], ("z", n % NZ)
            SPn, sr = SPt[n % NSP], ("SP", n % NSP)
            SPR, spr = Spre[h], ("Spre", h)
            c1 = c0 + 128 if j >= 0 else 0
            if c1 < 512:
                em.op("pe", lambda: nc.tensor.matmul(z[:, c1:512], lhsT=nones[:, :], rhs=SPR[:, c1:512], start=False, stop=True,
                                                     skip_group_check=True),
                      reads=[spr, "nones"], writes=[zr])
            em.op("pe", lambda: nc.tensor.matmul(z[:, c0:512], lhsT=ltri[:, :], rhs=SPn[:, c0:512], start=False, stop=True,
                                                 skip_group_check=True),
                  reads=[sr, "ltri"], writes=[zr])
            if kb > 0:
                if j >= 0:
                    em.op("dve", lambda: nc.vector.tensor_copy(out=SPR[:, c0:c0 + 128], in_=SPn[:, c0:c0 + 128]), reads=[sr], writes=[spr])
                if c1 < 512:
                    em.op("dve", lambda: nc.vector.tensor_tensor(out=SPR[:, c1:512], in0=SPR[:, c1:512], in1=SPn[:, c1:512], op=ALU.add),
                          reads=[sr, spr], writes=[spr])

        def s4_p(n):
            h, i, kb, j, c0 = geom(steps[n])
            z, zr = zb[n % NZ], ("z", n % NZ)
            P, pr = PBt[n % NPB], ("PB", n % NPB)
            em.op("act", lambda: nc.scalar.activation(out=P[:, c0:512], in_=z[:, c0:512], func=AF.Exp), reads=[zr], writes=[pr])
            if j >= 0:
                if c0 > 0:
                    em.op("pool", lambda: nc.gpsimd.memset(P[:, 0:c0], 0.0), writes=[pr])
                em.op("dve", lambda: nc.vector.tensor_tensor(out=P[:, c0:c0 + 128], in0=P[:, c0:c0 + 128], in1=mtri[:, :], op=ALU.mult),
                      reads=[pr, "mtri"], writes=[pr])

        def s5_pv(n):
            h, i, kb, j, c0 = geom(steps[n])
            P, pr = PBt[n % NPB], ("PB", n % NPB)
            first = (kb == 4 * i + 3)
            em.op("pe", lambda: nc.tensor.matmul(po[h][:, :], lhsT=VB[:, kb, h * 64:(h + 1) * 64], rhs=P[:, :], start=first, stop=(kb == 0)),
                  reads=[pr, "VB"], writes=[("po", h)])
            if kb == 0:
                em.op("dve", lambda: nc.vector.tensor_copy(out=obst[h][:, :], in_=po[h][:, :]), reads=[("po", h)], writes=[("obst", h)])
                em.dma("pool", [(o_d[ob_row0 + h * 64:ob_row0 + (h + 1) * 64, i * 512:(i + 1) * 512], obst[h][:, :])], f"D_obst{h}",
                       reads=[("obst", h)], writes=[("ob_out", h, i)])

        def a_items():
            NCH = S // CH
            li = 0
            for c in range(NCH):
                for g, (w, d) in enumerate(DSW):
                    sl = li % 2
                    li += 1
                    nper = 16 // d
                    tok0 = CH * c
                    em.dma("sp", [(QA[sl][:, :], qA(g)[:, tok0:tok0 + CH])], f"D_QA{sl}", writes=[("QA", sl)])
                    kp = []
                    if c > 0:
                        kp.append((KA[sl][:, 0:CH], kA(g)[:, tok0 - CH:tok0]))
                    kp.append((KA[sl][:, CH:2 * CH], kA(g)[:, tok0:tok0 + CH]))
                    em.dma("sp", kp, f"D_KA{sl}", writes=[("KA", sl)])
                    vp = []
                    for r in range(d):
                        m_first = (tok0 // d) - 128
                        nn0 = 0
                        if c == 0:
                            m_first += 128
                            nn0 = 1
                        nsl = nper + 1 - nn0
                        src = vA(g)[m_first * d + r:(m_first + nsl * 128 - 1) * d + r + 1:d, :]
                        src = src.rearrange("(n p) (h c) -> p n h c", p=128, h=2)
                        base = r * (nper + 1) + nn0
                        for hh in range(2):
                            vp.append((VA[sl][:, base:base + nsl, hh, 0:64], src[:, :, hh, :]))
                    em.dma("sp", vp, f"D_VA{sl}", writes=[("VA", sl)])
                    yield
                    Kv = KA[sl][:, :].rearrange("p (m dd) -> p m dd", dd=d)
                    Qv = QA[sl][:, :].rearrange("p (m dd) -> p m dd", dd=d)
                    descs = []
                    blocks = [(r, nn) for r in range(d) for nn in range(nper)]
                    for j in range(2):
                        for bi in range(0, len(blocks), 2):
                            descs.append((j, blocks[bi:bi + 2]))

                    def hasp(nn):
                        return not (c == 0 and nn == 0)

                    def st_s(k):
                        j, pair = descs[k]
                        hp = slice(j * 64, (j + 1) * 64)
                        sa, sar = sA[k % 2], ("sA", k % 2)
                        for qi, (r, nn) in enumerate(pair):
                            mq = nn * 128
                            if hasp(nn):
                                em.op("pe", lambda qi=qi, mq=mq, r=r: nc.tensor.matmul(
                                    sa[:, qi * 256:qi * 256 + 128], lhsT=Kv[hp, CH // d + mq - 128:CH // d + mq, r],
                                    rhs=Qv[hp, mq:mq + 128, r], start=True, stop=True),
                                    reads=[("QA", sl), ("KA", sl)], writes=[sar], sig=False)
                            em.op("pe", lambda qi=qi, mq=mq, r=r: nc.tensor.matmul(
                                sa[:, qi * 256 + 128:qi * 256 + 256], lhsT=Kv[hp, CH // d + mq:CH // d + mq + 128, r],
                                rhs=Qv[hp, mq:mq + 128, r], start=True, stop=True),
                                reads=[("QA", sl), ("KA", sl)], writes=[sar], sig=(qi == len(pair) - 1))

                    def st_e(k):
                        j, pair = descs[k]
                        sa, sar = sA[k % 2], ("sA", k % 2)
                        T, tr = TA[k % 2], ("TA", k % 2)
                        PAt, par = PA[k % 2], ("PA", k % 2)
                        lo = 0 if hasp(pair[0][1]) else 128
                        hi = 256 * len(pair)
                        em.op("dve", lambda: nc.vector.tensor_tensor(out=T[:, lo:hi], in0=sa[:, lo:hi], in1=biasA[:, g * 2 + j, lo:hi], op=ALU.add),
                              reads=[sar, "biasA"], writes=[tr])
                        em.op("act", lambda: nc.scalar.activation(out=PAt[:, lo:hi], in_=T[:, lo:hi], func=AF.Exp),
                              reads=[tr], writes=[par])

                    def st_o(k):
                        j, pair = descs[k]
                        PAt, par = PA[k % 2], ("PA", k % 2)
                        oa, oar = oA[k % 2], ("oA", k % 2)
                        accv = acc[j][:, :].rearrange("p (m dd) -> p m dd", dd=d)
                        for qi, (r, nn) in enumerate(pair):
                            slot = r * (nper + 1) + nn
                            last = (qi == len(pair) - 1)
                            if hasp(nn):
                                em.op("pe", lambda qi=qi, slot=slot: nc.tensor.matmul(
                                    oa[0:65, qi * 128:(qi + 1) * 128], lhsT=VA[sl][:, slot, j, :], rhs=PAt[:, qi * 256:qi * 256 + 128],
                                    start=True, stop=False), reads=[par, ("VA", sl)], writes=[oar], sig=False)
                            em.op("pe", lambda qi=qi, slot=slot, nn=nn: nc.tensor.matmul(
                                oa[0:65, qi * 128:(qi + 1) * 128], lhsT=VA[sl][:, slot + 1, j, :], rhs=PAt[:, qi * 256 + 128:qi * 256 + 256],
                                start=(not hasp(nn)), stop=True), reads=[par, ("VA", sl)], writes=[oar], sig=last)
                        for qi, (r, nn) in enumerate(pair):
                            dst = accv[:, nn * 128:(nn + 1) * 128, r]
                            if g == 0:
                                em.op("dve", lambda qi=qi, dst=dst: nc.vector.tensor_copy(out=dst, in_=oa[0:65, qi * 128:(qi + 1) * 128]),
                                      reads=[oar], writes=[("acc", j)])
                            else:
                                em.op("dve", lambda qi=qi, dst=dst: nc.vector.tensor_tensor(out=dst, in0=dst, in1=oa[0:65, qi * 128:(qi + 1) * 128], op=ALU.add),
                                      reads=[oar, ("acc", j)], writes=[("acc", j)])

                    ND = len(descs)
                    for k in range(-2, ND):
                        if 0 <= k + 2 < ND:
                            st_s(k + 2)
                        if 0 <= k + 1 < ND:
                            st_e(k + 1)
                        if 0 <= k < ND:
                            st_o(k)
                    yield
                for j in range(2):
                    em.dma("pool", [(den_d[j:j + 1, tok0:tok0 + CH], acc[j][64:65, :])], f"D_den{j}", reads=[("acc", j)], writes=[("den", j)])
                    em.dma("pool", [(dbc[j][:, :], den_d[j:j + 1, tok0:tok0 + CH].partition_broadcast(64))], f"D_dbc{j}",
                           reads=[("den", j)], writes=[("dbc", j)])
                    em.op("dve", lambda j=j: nc.vector.reciprocal(out=dbc[j][:, :], in_=dbc[j][:, :]), reads=[("dbc", j)], writes=[("dbc", j)])
                    em.op("dve", lambda j=j: nc.vector.tensor_tensor(out=oast[j][:, :], in0=acc[j][0:64, :], in1=dbc[j][:, :], op=ALU.mult),
                          reads=[("dbc", j), ("acc", j)], writes=[("oast", j)])
                    em.dma("pool", [(o_d[oa_row0 + j * 64:oa_row0 + (j + 1) * 64, tok0:tok0 + CH], oast[j][:, :])], f"D_oast{j}",
                           reads=[("oast", j)], writes=[("oa_out", j, c)])
                    yield

        NS = len(steps)
        agen = a_items()
        A_EVERY = 10 ** 9
        for n in range(-3, NS):
            if 0 <= n + 3 < NS:
                s0_z(n + 3)
            if 0 <= n + 2 < NS:
                s1_exp(n + 2)
            if 0 <= n + 1 < NS:
                s2_ln(n + 1)
                s3_cum(n + 1)
            if 0 <= n < NS:
                s4_p(n)
                s5_pv(n)
            if n >= 0 and n % A_EVERY == 0:
                next(agen, None)
        em.barrier()
        for _ in agen:
            pass
        em.barrier()


def _pc(v, n):
    return np.ascontiguousarray(np.asarray(v, np.float32).reshape(n, 128).T)


def _run(nc, in_maps):
    res = run_bass_kernel_spmd(nc, in_maps, core_ids=list(range(NCORES)))
    return res.results


def kernel_unfused(x, norm1, w_in, b_gate, w_br, w_o, norm2, w_up, conv_w, conv_b, w_down, norm_f):
    f32 = lambda a: np.ascontiguousarray(np.asarray(a, dtype=np.float32))
    x = f32(x)
    norm1, w_in, b_gate, w_br, w_o = f32(norm1), f32(w_in), f32(b_gate), f32(w_br), f32(w_o)
    norm2, w_up, conv_w, conv_b, w_down, norm_f = f32(norm2), f32(w_up), f32(conv_w), f32(conv_b), f32(w_down), f32(norm_f)
    ident = _bf16(np.eye(128))
    nc1 = build_p1()
    nc2 = build_p2()
    consts = [p2_consts(p) for p in range(2)]
    gfb = np.ascontiguousarray(np.broadcast_to(norm_f, (128, D_MODEL)))
    xcur = [x[c // 2, (c % 2) * TOK:(c % 2 + 1) * TOK] for c in range(NCORES)]
    for l in range(DEPTH):
        wqkv = np.ascontiguousarray(w_in[l][:, :3072])
        g1 = _pc(norm1[l], 8)
        r1 = _run(nc1, [{"x": np.ascontiguousarray(xcur[c]), "w": wqkv, "g": g1, "ident": ident} for c in range(NCORES)])
        ins2 = []
        for c in range(NCORES):
            b, p = c // 2, c % 2
            qk = [np.asarray(r1[2 * b + h]["qk"]) for h in range(2)]
            v = [np.asarray(r1[2 * b + h]["v"]) for h in range(2)]
            cat = lambda ch: np.concatenate([qk[0][ch], qk[1][ch]], axis=1)
            qkA = np.stack([np.stack([cat(2 * g + p), cat(6 + 2 * g + p)]) for g in range(3)])
            qkB = np.stack([cat(12 + p), cat(14 + p)])
            vcat = np.concatenate(v, axis=0)
            vA = np.stack([vcat[:, (g * 4 + 2 * p) * 64:(g * 4 + 2 * p) * 64 + 128] for g in range(3)])
            vB = vcat[:, 768 + 2 * p * 64:768 + 2 * p * 64 + 128]
            ins2.append(dict(qkA=np.ascontiguousarray(qkA), qkB=np.ascontiguousarray(qkB), vA=np.ascontiguousarray(vA),
                             vB=np.ascontiguousarray(vB), **consts[p]))
        r2 = _run(nc2, ins2)
        nc3 = build_p3(l == DEPTH - 1)
        ins3 = []
        for c in range(NCORES):
            b, h = c // 2, c % 2
            o0, o1 = np.asarray(r2[2 * b]["oT"]), np.asarray(r2[2 * b + 1]["oT"])
            oT = np.concatenate([o0[0:128], o1[0:128], o0[128:256], o1[128:256]], axis=0)
            xin = np.zeros((TOKP, D_MODEL), np.float32)
            oin = np.zeros((512, TOKP), dtype=oT.dtype)
            xin[128:] = xcur[c]
            oin[:, 128:] = oT[:, h * TOK:(h + 1) * TOK]
            if h == 1:
                xin[:128] = xcur[c - 1][TOK - 128:]
                oin[:, :128] = oT[:, TOK - 128:TOK]
            ins3.append(dict(x=xin, oT=oin, wg=np.ascontiguousarray(w_in[l][:, 3072:]), bg=_pc(b_gate[l], 16), g1=g1,
                             wbr=w_br[l], wo=w_o[l], g2=_pc(norm2[l], 8), wup=w_up[l],
                             cw=np.ascontiguousarray(conv_w[l].reshape(3, 22, 128).transpose(2, 1, 0)),
                             cb=_pc(conv_b[l], 22), wdn=w_down[l], gf=gfb, ident=ident))
        r3 = _run(nc3, ins3)
        xcur = [np.asarray(r3[c]["xo"])[128:] for c in range(NCORES)]
    out = np.empty((BATCH, SEQ, D_MODEL), np.float32)
    for c in range(NCORES):
        out[c // 2, (c % 2) * TOK:(c % 2 + 1) * TOK] = xcur[c]
    return out


NTOKF = 8448
LNAMES = ["wqkv", "g1", "wg", "bg", "wbr", "wo", "g2", "wup", "cw", "cb", "wdn"]
LSHAPES = {"wqkv": [D_MODEL, 3072], "g1": [128, 8], "wg": [D_MODEL, 2048], "bg": [128, 16], "wbr": [512, D_MODEL],
           "wo": [D_MODEL, D_MODEL], "g2": [128, 8], "wup": [D_MODEL, 2 * D_FF], "cw": [128, 22, 3], "cb": [128, 22],
           "wdn": [D_FF, D_MODEL]}


def build_fused():
    nc = bass.Bass("TRN2", target_bir_lowering=False)
    ext = lambda name, shape, dt: nc.dram_tensor(name, shape, dt, kind="ExternalInput").ap()
    x_d = ext("x", [NTOKF, D_MODEL], F32)
    W = [{n: ext(f"{n}{l}", LSHAPES[n], F32) for n in LNAMES} for l in range(DEPTH)]
    gf_d = ext("gf", [128, D_MODEL], F32)
    id_d = ext("ident", [128, 128], BF16)
    bias_d = [ext(f"biasA{p}", [128, 6, 512], F32) for p in range(2)]
    mtri_d = ext("mtri", [128, 128], BF16)
    ltri_d = ext("ltri", [128, 128], BF16)
    nones_d = ext("nones", [128, 128], BF16)
    xo_d = nc.dram_tensor("xo", [NTOKF, D_MODEL], F32, kind="ExternalOutput").ap()
    qk = nc.dram_tensor("qk_scr", [16, 128, SEQ], BF16).ap()
    v = nc.dram_tensor("v_scr", [SEQ, 1024], BF16).ap()
    oT = nc.dram_tensor("oT_scr", [512, NTOKF], BF16).ap()
    den = nc.dram_tensor("den_scr", [2, SEQ], F32).ap()
    xmid = nc.dram_tensor("xmid_scr", [NTOKF, D_MODEL], F32).ap()
    x1 = nc.dram_tensor("x1_scr", [NTOKF, D_MODEL], F32).ap()
    em = Em(nc)
    for l in range(DEPTH):
        xin = x_d if l == 0 else x1
        xout = x1 if l < DEPTH - 1 else xo_d
        w = W[l]
        emit_p1(nc, em, f"L{l}a", xin[0:SEQ, :], w["wqkv"], w["g1"], id_d, qk, v, SEQ)
        for p in range(2):
            emit_p2(nc, em, f"L{l}b{p}",
                    lambda g, p=p: qk[2 * g + p, :, :], lambda g, p=p: qk[6 + 2 * g + p, :, :],
                    lambda g, p=p: v[:, (g * 4 + 2 * p) * 64:(g * 4 + 2 * p) * 64 + 128],
                    qk[12 + p, :, :], qk[14 + p, :, :], v[:, 768 + 2 * p * 64:768 + 2 * p * 64 + 128],
                    bias_d[p], mtri_d, ltri_d, nones_d, oT, p * 128, 256 + p * 128, den, SEQ)
        emit_p3(nc, em, f"L{l}c", l == DEPTH - 1, xin, oT, w["wg"], w["bg"], w["g1"], w["wbr"], w["wo"], w["g2"], w["wup"],
                w["cw"], w["cb"], w["wdn"], gf_d, id_d, xout, xmid, NTOKF)
    em.finish("sp")
    return nc


ACTIVE_CORES = [0, 1, 4, 5]


def kernel(x, norm1, w_in, b_gate, w_br, w_o, norm2, w_up, conv_w, conv_b, w_down, norm_f):
    f32 = lambda a: np.ascontiguousarray(np.asarray(a, dtype=np.float32))
    x = f32(x)
    norm1, w_in, b_gate, w_br, w_o = f32(norm1), f32(w_in), f32(b_gate), f32(w_br), f32(w_o)
    norm2, w_up, conv_w, conv_b, w_down, norm_f = f32(norm2), f32(w_up), f32(conv_w), f32(conv_b), f32(w_down), f32(norm_f)
    shared = {"gf": np.ascontiguousarray(np.broadcast_to(norm_f, (128, D_MODEL))), "ident": _bf16(np.eye(128))}
    for p in range(2):
        c = p2_consts(p)
        shared[f"biasA{p}"] = c["biasA"]
        shared.update({k: c[k] for k in ("mtri", "ltri", "nones")})
    for l in range(DEPTH):
        shared.update({f"wqkv{l}": np.ascontiguousarray(w_in[l][:, :3072]), f"g1{l}": _pc(norm1[l], 8),
                       f"wg{l}": np.ascontiguousarray(w_in[l][:, 3072:]), f"bg{l}": _pc(b_gate[l], 16),
                       f"wbr{l}": w_br[l], f"wo{l}": w_o[l], f"g2{l}": _pc(norm2[l], 8), f"wup{l}": w_up[l],
                       f"cw{l}": np.ascontiguousarray(conv_w[l].reshape(3, 22, 128).transpose(2, 1, 0)),
                       f"cb{l}": _pc(conv_b[l], 22), f"wdn{l}": w_down[l]})
    nc = build_fused()
    zeros = {k: np.zeros_like(v) for k, v in shared.items()}
    ins = []
    for c in range(NCORES):
        xin = np.zeros((NTOKF, D_MODEL), np.float32)
        if c in ACTIVE_CORES:
            xin[:SEQ] = x[ACTIVE_CORES.index(c)]
            ins.append(dict(shared, x=xin))
        else:
            ins.append(dict(zeros, x=xin))
    res = _run(nc, ins)
    out = np.empty((BATCH, SEQ, D_MODEL), np.float32)
    for b in range(BATCH):
        out[b] = np.asarray(res[ACTIVE_CORES[b]]["xo"])[:SEQ]
    return out
```

```python
from contextlib import ExitStack
import numpy as np
import ml_dtypes
import concourse.bass as bass
import concourse.mybir as mybir
from concourse.bass_utils import run_bass_kernel_spmd

F32 = mybir.dt.float32
BF16 = mybir.dt.bfloat16
AF = mybir.ActivationFunctionType
ALU = mybir.AluOpType
AX = mybir.AxisListType

D_MODEL = 1024
BATCH = 4
SEQ = 8192
DEPTH = 2
HD = 64
D_FF = 2816
RMS_EPS = 1e-6
NCORES = 8
TOK = SEQ // 2
NEG = -30000.0


class Em:
    def __init__(self, nc):
        self.nc = nc
        self.eng = {"pe": nc.tensor, "act": nc.scalar, "dve": nc.vector, "pool": nc.gpsimd, "sp": nc.sync}
        self.sems = {}
        self.cnt = {}
        self.waited = {e: {} for e in self.eng}
        self.last_w = {}
        self.readers = {}
        self.ekey = {}
        self.gen = 0
        for e in self.eng:
            self.ekey[e] = "E_" + e
            self._mksem("E_" + e)

    def _mksem(self, key):
        if key not in self.sems:
            self.sems[key] = self.nc.alloc_semaphore(name=key)
            self.cnt[key] = 0
        return self.sems[key]

    def _wait(self, e, tok):
        key, val = tok
        if self.waited[e].get(key, 0) >= val:
            return
        self.eng[e].wait_ge(self.sems[key], val)
        self.waited[e][key] = val

    def _deps(self, e, reads, writes, skip_same=False):
        own = self.ekey[e]
        toks = []
        for r in reads:
            if r in self.last_w:
                toks.append(self.last_w[r])
        for w in writes:
            if w in self.last_w:
                toks.append(self.last_w[w])
            toks.extend(self.readers.get(w, ()))
        for t in toks:
            if skip_same and t[0] == own:
                continue
            self._wait(e, t)

    def _record(self, tok, reads, writes):
        for r in reads:
            self.readers.setdefault(r, []).append(tok)
        for w in writes:
            self.last_w[w] = tok
            self.readers[w] = []

    def op(self, e, fn, reads=(), writes=(), sig=True):
        self._deps(e, reads, writes, skip_same=(e == "pe"))
        ins = fn()
        key = self.ekey[e]
        if sig:
            self.cnt[key] += 1
            ins.then_inc(self.sems[key], 1)
            tok = (key, self.cnt[key])
        else:
            tok = (key, self.cnt[key] + 1)
        self._record(tok, reads, writes)
        return ins

    def dma(self, q, pairs, semkey, reads=(), writes=()):
        self._mksem(semkey)
        self._deps(q, reads, writes)
        for (o, i) in pairs:
            self.cnt[semkey] += 16
            self.eng[q].dma_start(out=o, in_=i).then_inc(self.sems[semkey], 16)
        tok = (semkey, self.cnt[semkey])
        self._record(tok, reads, writes)
        return tok

    def barrier(self):
        for e in self.eng:
            for key in self.sems:
                if self.cnt[key] > 0:
                    self._wait(e, (key, self.cnt[key]))
        for e in self.eng:
            if self.cnt[self.ekey[e]] > 6000:
                self.gen += 1
                self.ekey[e] = f"E_{e}_{self.gen}"
                self._mksem(self.ekey[e])

    def finish(self, e="sp"):
        for tok in list(self.last_w.values()):
            self._wait(e, tok)
        for lst in self.readers.values():
            for tok in lst:
                self._wait(e, tok)


def _bf16(a):
    return np.asarray(a, dtype=np.float32).astype(ml_dtypes.bfloat16)


def emit_norm_transpose(em, nc, xs, blk_res, hb, hT, tp_ps, ident, ssq, rstd, nhalf, junk, tag, nblk, cpy_eng="act"):
    st = tag[0]
    for b in range(nblk):
        em.op("act", lambda b=b: nc.scalar.activation(out=junk[:, :], in_=xs[:, b, :], func=AF.Square,
                                                      accum_out=ssq[:, b:b + 1]),
              reads=[blk_res[b]], writes=[(st, "junk"), (st, "ssq", b)])
    em.op("pool", lambda: nc.gpsimd.tensor_scalar(out=rstd[:, 0:nblk], in0=ssq[:, 0:nblk], scalar1=1.0 / D_MODEL,
                                                  scalar2=RMS_EPS, op0=ALU.mult, op1=ALU.add),
          reads=[(st, "ssq", b) for b in range(nblk)], writes=[(st, "rstd")])
    em.op("pool", lambda: nc.gpsimd.tensor_tensor(out=rstd[:, 0:nblk], in0=rstd[:, 0:nblk], in1=nhalf[:, 0:nblk], op=ALU.pow),
          reads=[(st, "rstd"), "nhalf"], writes=[(st, "rstd")])
    for b in range(nblk):
        em.op("dve", lambda b=b: nc.vector.tensor_scalar(out=hb[:, b, :], in0=xs[:, b, :], scalar1=rstd[:, b:b + 1],
                                                         scalar2=None, op0=ALU.mult),
              reads=[blk_res[b], (st, "rstd")], writes=[(st, "hb", b)])
    for b in range(nblk):
        ps = tp_ps[b % len(tp_ps)]
        psr = (st, "tp", b % len(tp_ps))
        for kc in range(8):
            em.op("pe", lambda b=b, kc=kc, ps=ps: nc.tensor.transpose(out=ps[:, kc, :], in_=hb[:, b, kc * 128:(kc + 1) * 128],
                                                                       identity=ident[:, :]),
                  reads=[(st, "hb", b), "ident"], writes=[psr], sig=(kc == 7))
        if cpy_eng == "act":
            em.op("act", lambda b=b, ps=ps: nc.scalar.copy(out=hT[:, :, b * 128:(b + 1) * 128], in_=ps[:, :, :]),
                  reads=[psr], writes=[(tag, "hT", b)])
        else:
            em.op("dve", lambda b=b, ps=ps: nc.vector.tensor_copy(out=hT[:, :, b * 128:(b + 1) * 128], in_=ps[:, :, :]),
                  reads=[psr], writes=[(tag, "hT", b)])


CAST_CYCLE = ["dve", "act", "dve", "act", "pool"]


def emit_load_w(em, nc, q, w_dram, nk, pieces, Wb, gvec, stages, res, ctr=[0]):
    for kc in range(nk):
        for (c0, n, d0, sc) in pieces:
            i = ctr[0] % len(stages)
            st, sres, skey = stages[i]
            qq = q if (ctr[0] % 2 == 0) else "act"
            em.dma(qq, [(st[:, 0:n], w_dram[kc * 128:(kc + 1) * 128, c0:c0 + n])], skey, writes=sres)
            eng = CAST_CYCLE[ctr[0] % len(CAST_CYCLE)]
            ctr[0] += 1
            if eng == "act" and gvec is not None and float(sc) != 1.0:
                eng = "dve"
            if eng == "act":
                scale = float(sc) if gvec is None else gvec[:, kc:kc + 1]
                em.op("act", lambda st=st, n=n, d0=d0, kc=kc, scale=scale: nc.scalar.activation(
                    out=Wb[:, kc, d0:d0 + n], in_=st[:, 0:n], func=AF.Copy, scale=scale),
                    reads=list(sres) + (["gvec"] if gvec is not None else []), writes=[(res, kc, d0)])
                continue
            E = nc.vector if eng == "dve" else nc.gpsimd
            if gvec is None:
                em.op(eng, lambda E=E, st=st, n=n, d0=d0, kc=kc, sc=sc: E.tensor_scalar(
                    out=Wb[:, kc, d0:d0 + n], in0=st[:, 0:n], scalar1=float(sc), scalar2=None, op0=ALU.mult),
                    reads=sres, writes=[(res, kc, d0)])
            else:
                em.op(eng, lambda E=E, st=st, n=n, d0=d0, kc=kc, sc=sc: E.tensor_scalar(
                    out=Wb[:, kc, d0:d0 + n], in0=st[:, 0:n], scalar1=gvec[:, kc:kc + 1], scalar2=float(sc),
                    op0=ALU.mult, op1=ALU.mult),
                    reads=list(sres) + ["gvec"], writes=[(res, kc, d0)])


P1_COLMAP = [(0, 768, 0, 0.125), (768, 768, 768, 1.0), (2304, 256, 1536, 0.125), (2560, 256, 1792, 1.0),
             (1536, 768, 2048, 1.0), (2816, 256, 2816, 1.0)]


def build_p1(ntok=TOK):
    nc = bass.Bass("TRN2", target_bir_lowering=False)
    x_d = nc.dram_tensor("x", [ntok, D_MODEL], F32, kind="ExternalInput").ap()
    w_d = nc.dram_tensor("w", [D_MODEL, 3072], F32, kind="ExternalInput").ap()
    g_d = nc.dram_tensor("g", [128, 8], F32, kind="ExternalInput").ap()
    id_d = nc.dram_tensor("ident", [128, 128], BF16, kind="ExternalInput").ap()
    qk_d = nc.dram_tensor("qk", [16, 128, ntok], BF16, kind="ExternalOutput").ap()
    v_d = nc.dram_tensor("v", [ntok, 1024], BF16, kind="ExternalOutput").ap()
    em = Em(nc)
    emit_p1(nc, em, "p1", x_d, w_d, g_d, id_d, qk_d, v_d, ntok)
    em.finish("sp")
    return nc


def emit_p1(nc, em, pfx, x_d, w_d, g_d, id_d, qk_d, v_d, ntok):
    NT = ntok // 512
    with ExitStack() as stk:
        SB = lambda name, shape, dt: stk.enter_context(nc.sbuf_tensor(pfx + "s_" + name, shape, dt))
        PS = lambda name, shape, dt: stk.enter_context(nc.psum_tensor(pfx + "p_" + name, shape, dt))
        Wb = SB("Wb", [128, 8, 3072], BF16)
        wst0 = SB("wst0", [128, 3072], F32)
        wst1 = SB("wst1", [128, 3072], F32)
        xs0 = SB("xs0", [128, 4, 1024], F32)
        xs1 = SB("xs1", [128, 4, 1024], F32)
        hb = SB("hb", [128, 4, 1024], BF16)
        hT0 = SB("hT0", [128, 8, 512], BF16)
        hT1 = SB("hT1", [128, 8, 512], BF16)
        junk = SB("junk", [128, 1024], BF16)
        ssq = SB("ssq", [128, 4], F32)
        rstd = SB("rstd", [128, 4], F32)
        nhalf = SB("nhalf", [128, 4], F32)
        gv = SB("gv", [128, 8], F32)
        ident = SB("ident", [128, 128], BF16)
        qst0 = SB("qst0", [128, 512], BF16)
        qst1 = SB("qst1", [128, 512], BF16)
        qst2 = SB("qst2", [128, 512], BF16)
        qst3 = SB("qst3", [128, 512], BF16)
        vst0 = SB("vst0", [128, 4, 1024], BF16)
        vst1 = SB("vst1", [128, 4, 1024], BF16)
        tp0 = PS("tp0", [128, 8, 128], BF16)
        tp1 = PS("tp1", [128, 8, 128], BF16)
        mm0 = PS("mm0", [128, 512], F32)
        mm1 = PS("mm1", [128, 512], F32)
        mm2 = PS("mm2", [128, 512], F32)
        mm3 = PS("mm3", [128, 512], F32)
        xs = [xs0, xs1]
        hTs = [hT0, hT1]
        qst = [qst0, qst1, qst2, qst3]
        vst = [vst0, vst1]
        mm = [mm0, mm1, mm2, mm3]
        em.op("pool", lambda: nc.gpsimd.memset(nhalf[:, :], -0.5), writes=["nhalf"])
        em.dma("sp", [(gv[:, :], g_d[:, :])], "D_gv", writes=["gvec"])
        em.dma("sp", [(ident[:, :], id_d[:, :])], "D_id", writes=["ident"])

        def load_x(ti):
            s = ti % 2
            src = x_d[ti * 512:(ti + 1) * 512, :].rearrange("(b p) f -> p b f", p=128)
            em.dma("sp", [(xs[s][:, b, :], src[:, b, :]) for b in range(4)], f"D_xs{s}",
                   writes=[("xs", s, b) for b in range(4)])

        load_x(0)
        emit_load_w(em, nc, "sp", w_d, 8, P1_COLMAP, Wb, gv,
                    [(wst0, [("wst", 0)], "D_wst0"), (wst1, [("wst", 1)], "D_wst1")], "Wb")
        wres = [("Wb", kc, d0) for kc in range(8) for (_, _, d0, _) in P1_COLMAP]
        mmi = 0
        qi = 0
        for ti in range(NT):
            s = ti % 2
            if ti + 1 < NT:
                load_x(ti + 1)
            hT = hTs[s]
            tag = ("n", s)
            emit_norm_transpose(em, nc, xs[s], [("xs", s, b) for b in range(4)], hb, hT, [tp0, tp1], ident, ssq, rstd,
                                nhalf, junk, tag, 4)
            hres = [(tag, "hT", b) for b in range(4)]
            for ch in range(16):
                ps = mm[mmi % 4]
                pr = ("mm", mmi % 4)
                mmi += 1
                for kc in range(8):
                    em.op("pe", lambda ps=ps, kc=kc, ch=ch: nc.tensor.matmul(
                        ps[:, :], lhsT=Wb[:, kc, ch * 128:(ch + 1) * 128], rhs=hT[:, kc, :], start=(kc == 0), stop=(kc == 7)),
                        reads=wres + hres if kc == 0 else (), writes=[pr], sig=(kc == 7))
                st = qst[qi % 4]
                sr = ("qst", qi % 4)
                qi += 1
                if ch % 2 == 0:
                    em.op("act", lambda st=st, ps=ps: nc.scalar.copy(out=st[:, :], in_=ps[:, :]), reads=[pr], writes=[sr])
                else:
                    em.op("dve", lambda st=st, ps=ps: nc.vector.tensor_copy(out=st[:, :], in_=ps[:, :]), reads=[pr], writes=[sr])
                em.dma("pool", [(qk_d[ch, :, ti * 512:(ti + 1) * 512], st[:, :])], f"D_qst{(qi - 1) % 4}",
                       reads=[sr], writes=[("qk_out", ch, ti)])
            vs = vst[s]
            for b in range(4):
                for half in range(2):
                    ps = mm[mmi % 4]
                    pr = ("mm", mmi % 4)
                    mmi += 1
                    for kc in range(8):
                        em.op("pe", lambda ps=ps, kc=kc, b=b, half=half: nc.tensor.matmul(
                            ps[:, :], lhsT=hT[:, kc, b * 128:(b + 1) * 128], rhs=Wb[:, kc, 2048 + half * 512:2048 + (half + 1) * 512],
                            start=(kc == 0), stop=(kc == 7)),
                            reads=wres + hres if kc == 0 else (), writes=[pr], sig=(kc == 7))
                    if half == 0:
                        em.op("act", lambda vs=vs, ps=ps, b=b: nc.scalar.copy(out=vs[:, b, 0:512], in_=ps[:, :]),
                              reads=[pr], writes=[("vst", s, b, 0)])
                    else:
                        em.op("dve", lambda vs=vs, ps=ps, b=b: nc.vector.tensor_copy(out=vs[:, b, 512:1024], in_=ps[:, :]),
                              reads=[pr], writes=[("vst", s, b, 1)])
            dst = v_d[ti * 512:(ti + 1) * 512, :].rearrange("(b p) f -> p b f", p=128)
            em.dma("pool", [(dst[:, b, :], vs[:, b, :]) for b in range(4)], f"D_vst{s}",
                   reads=[("vst", s, b, h) for b in range(4) for h in range(2)], writes=[("v_out", ti)])
        em.barrier()


NB3 = 3
TOKP = TOK + 128


def build_p3(final, ntok=TOKP, stage=2):
    nc = bass.Bass("TRN2", target_bir_lowering=False)
    x_d = nc.dram_tensor("x", [ntok, D_MODEL], F32, kind="ExternalInput").ap()
    o_d = nc.dram_tensor("oT", [512, ntok], BF16, kind="ExternalInput").ap()
    wg_d = nc.dram_tensor("wg", [D_MODEL, 2048], F32, kind="ExternalInput").ap()
    bg_d = nc.dram_tensor("bg", [128, 16], F32, kind="ExternalInput").ap()
    g1_d = nc.dram_tensor("g1", [128, 8], F32, kind="ExternalInput").ap()
    wbr_d = nc.dram_tensor("wbr", [512, D_MODEL], F32, kind="ExternalInput").ap()
    wo_d = nc.dram_tensor("wo", [D_MODEL, D_MODEL], F32, kind="ExternalInput").ap()
    g2_d = nc.dram_tensor("g2", [128, 8], F32, kind="ExternalInput").ap()
    wup_d = nc.dram_tensor("wup", [D_MODEL, 2 * D_FF], F32, kind="ExternalInput").ap()
    cw_d = nc.dram_tensor("cw", [128, 22, 3], F32, kind="ExternalInput").ap()
    cb_d = nc.dram_tensor("cb", [128, 22], F32, kind="ExternalInput").ap()
    wdn_d = nc.dram_tensor("wdn", [D_FF, D_MODEL], F32, kind="ExternalInput").ap()
    gf_d = nc.dram_tensor("gf", [128, D_MODEL], F32, kind="ExternalInput").ap()
    id_d = nc.dram_tensor("ident", [128, 128], BF16, kind="ExternalInput").ap()
    xo_d = nc.dram_tensor("xo", [ntok, D_MODEL], F32, kind="ExternalOutput").ap()
    xm_d = nc.dram_tensor("xmid", [ntok, D_MODEL], F32).ap()
    em = Em(nc)
    emit_p3(nc, em, "p3", final, x_d, o_d, wg_d, bg_d, g1_d, wbr_d, wo_d, g2_d, wup_d, cw_d, cb_d, wdn_d, gf_d, id_d, xo_d, xm_d, ntok, stage)
    em.finish("sp")
    return nc


def emit_p3(nc, em, pfx, final, x_d, o_d, wg_d, bg_d, g1_d, wbr_d, wo_d, g2_d, wup_d, cw_d, cb_d, wdn_d, gf_d, id_d, xo_d, xm_d, ntok, stage=2):
    NTK = NB3 * 128
    NT = ntok // NTK
    assert NT * NTK == ntok

    def tile_view(d, ti):
        return d[ti * NTK:(ti + 1) * NTK, :].rearrange("(b p) f -> p b f", p=128)

    with ExitStack() as stk:
        SB = lambda name, shape, dt: stk.enter_context(nc.sbuf_tensor(pfx + "a_" + name, shape, dt))
        PS = lambda name, shape, dt: stk.enter_context(nc.psum_tensor(pfx + "pa_" + name, shape, dt))
        Wg = SB("Wg", [128, 8, 2048], BF16)
        Wbr = SB("Wbr", [128, 4, 1024], BF16)
        Wo = SB("Wo", [128, 8, 1024], BF16)
        wst = [SB(f"wst{i}", [128, 2048], F32) for i in range(2)]
        xt = [SB(f"xt{i}", [128, NB3, 1024], F32) for i in range(2)]
        ot = [SB(f"ot{i}", [128, 4, NTK], BF16) for i in range(2)]
        hb = SB("hb", [128, NB3, 1024], BF16)
        hTs = [SB(f"hT{i}", [128, 8, NTK], BF16) for i in range(2)]
        gate = SB("gate", [128, 16, NTK], F32)
        mg = SB("mg", [128, 8, NTK], BF16)
        t1 = [SB(f"t1_{i}", [128, NTK], F32) for i in range(2)]
        t2 = [SB(f"t2_{i}", [128, NTK], F32) for i in range(2)]
        junk = SB("junk", [128, 1024], BF16)
        ssq = SB("ssq", [128, 4], F32)
        rstd = SB("rstd", [128, 4], F32)
        nhalf = SB("nhalf", [128, 4], F32)
        gv = SB("gv", [128, 8], F32)
        bg = SB("bg", [128, 16], F32)
        ident = SB("ident", [128, 128], BF16)
        tp = [PS(f"tp{i}", [128, 8, 128], BF16) for i in range(2)]
        mm = [PS(f"mm{i}", [128, 512], F32) for i in range(6)]
        stages = [(wst[i], [("wst", i)], f"D_awst{i}") for i in range(2)]

        em.op("pool", lambda: nc.gpsimd.memset(nhalf[:, :], -0.5), writes=["nhalf"])
        em.dma("sp", [(gv[:, :], g1_d[:, :])], "D_gv", writes=["gvec"])
        em.dma("sp", [(bg[:, :], bg_d[:, :])], "D_bg", writes=["bg"])
        em.dma("sp", [(ident[:, :], id_d[:, :])], "D_id", writes=["ident"])

        def load_a(ti):
            s = ti % 2
            src = tile_view(x_d, ti)
            em.dma("sp", [(xt[s][:, b, :], src[:, b, :]) for b in range(NB3)], f"D_axt{s}",
                   writes=[("xt", s, b) for b in range(NB3)])
            osrc = o_d[:, ti * NTK:(ti + 1) * NTK].rearrange("(c p) t -> p c t", p=128)
            em.dma("sp", [(ot[s][:, c, :], osrc[:, c, :]) for c in range(4)], f"D_aot{s}", writes=[("ot", s)])

        load_a(0)
        emit_load_w(em, nc, "sp", wg_d, 8, [(0, 2048, 0, 1.0)], Wg, gv, stages, "Wg")
        emit_load_w(em, nc, "sp", wbr_d, 4, [(0, 1024, 0, 1.0)], Wbr, None, stages, "Wbr")
        emit_load_w(em, nc, "sp", wo_d, 8, [(0, 1024, 0, 1.0)], Wo, None, stages, "Wo")
        wg_res = [("Wg", kc, 0) for kc in range(8)]
        wbr_res = [("Wbr", kc, 0) for kc in range(4)]
        wo_res = [("Wo", kc, 0) for kc in range(8)]
        mmi = [0]

        def next_mm():
            i = mmi[0] % len(mm)
            mmi[0] += 1
            return mm[i], ("mm", i)

        def norm_a(ti):
            s = ti % 2
            emit_norm_transpose(em, nc, xt[s], [("xt", s, b) for b in range(NB3)], hb, hTs[s], tp, ident, ssq, rstd, nhalf,
                                junk, ("na", s), NB3)

        norm_a(0)
        for ti in range(NT):
            s = ti % 2
            if ti + 1 < NT:
                load_a(ti + 1)
            hT = hTs[s]
            tag = ("na", s)
            xres = [("xt", s, b) for b in range(NB3)]
            hres = [(tag, "hT", b) for b in range(NB3)]
            for gc in range(16):
                ps, pr = next_mm()
                for kc in range(8):
                    em.op("pe", lambda ps=ps, kc=kc, gc=gc: nc.tensor.matmul(
                        ps[:, 0:NTK], lhsT=Wg[:, kc, gc * 128:(gc + 1) * 128], rhs=hT[:, kc, :], start=(kc == 0), stop=(kc == 7)),
                        reads=(wg_res + hres) if kc == 0 else (), writes=[pr], sig=(kc == 7))
                em.op("act", lambda ps=ps, gc=gc: nc.scalar.activation(out=gate[:, gc, :], in_=ps[:, 0:NTK], func=AF.Sigmoid,
                                                                      bias=bg[:, gc:gc + 1], scale=1.0),
                      reads=[pr, "bg"], writes=[("gate", gc)])
            for n in range(8):
                psA, prA = next_mm()
                for kc in range(2):
                    em.op("pe", lambda psA=psA, kc=kc, n=n: nc.tensor.matmul(
                        psA[:, 0:NTK], lhsT=Wbr[:, kc, n * 128:(n + 1) * 128], rhs=ot[s][:, kc, :], start=(kc == 0), stop=(kc == 1)),
                        reads=(wbr_res + [("ot", s)]) if kc == 0 else (), writes=[prA], sig=(kc == 1))
                psB, prB = next_mm()
                for kc in range(2, 4):
                    em.op("pe", lambda psB=psB, kc=kc, n=n: nc.tensor.matmul(
                        psB[:, 0:NTK], lhsT=Wbr[:, kc, n * 128:(n + 1) * 128], rhs=ot[s][:, kc, :], start=(kc == 2), stop=(kc == 3)),
                        reads=(wbr_res + [("ot", s)]) if kc == 2 else (), writes=[prB], sig=(kc == 3))
                j = n % 2
                em.op("dve", lambda psA=psA, n=n, j=j: nc.vector.tensor_tensor(out=t1[j][:, :], in0=gate[:, n, :], in1=psA[:, 0:NTK], op=ALU.mult),
                      reads=[prA, ("gate", n)], writes=[("t1", j)])
                em.op("dve", lambda psB=psB, n=n, j=j: nc.vector.tensor_tensor(out=t2[j][:, :], in0=gate[:, 8 + n, :], in1=psB[:, 0:NTK], op=ALU.mult),
                      reads=[prB, ("gate", 8 + n)], writes=[("t2", j)])
                em.op("pool", lambda n=n, j=j: nc.gpsimd.tensor_tensor(out=mg[:, n, :], in0=t1[j][:, :], in1=t2[j][:, :], op=ALU.add),
                      reads=[("t1", j), ("t2", j)], writes=[("mg", n)])
            if ti + 1 < NT:
                norm_a(ti + 1)
            mres = [("mg", n) for n in range(8)]
            for b in range(NB3):
                for half in range(2):
                    ps, pr = next_mm()
                    for kc in range(8):
                        em.op("pe", lambda ps=ps, kc=kc, b=b, half=half: nc.tensor.matmul(
                            ps[:, :], lhsT=mg[:, kc, b * 128:(b + 1) * 128], rhs=Wo[:, kc, half * 512:(half + 1) * 512],
                            start=(kc == 0), stop=(kc == 7)),
                            reads=(wo_res + mres) if kc == 0 else (), writes=[pr], sig=(kc == 7))
                    em.op("dve", lambda ps=ps, b=b, half=half: nc.vector.tensor_tensor(
                        out=xt[s][:, b, half * 512:(half + 1) * 512], in0=xt[s][:, b, half * 512:(half + 1) * 512], in1=ps[:, :], op=ALU.add),
                        reads=[pr, ("xt", s, b)], writes=[("xt", s, b)])
            dst = tile_view(xm_d if stage == 2 else xo_d, ti)
            em.dma("pool", [(dst[:, b, :], xt[s][:, b, :]) for b in range(NB3)], f"D_axo{s}",
                   reads=xres, writes=[("xmid", ti)])
        em.barrier()
    if stage == 1:
        return

    em2 = em
    with ExitStack() as stk:
        SB = lambda name, shape, dt: stk.enter_context(nc.sbuf_tensor(pfx + "b_" + name, shape, dt))
        PS = lambda name, shape, dt: stk.enter_context(nc.psum_tensor(pfx + "pb_" + name, shape, dt))
        Wup = SB("Wup", [128, 8, 2 * D_FF], BF16)
        Wdn = SB("Wdn", [128, 22, 1024], BF16)
        xt = [SB(f"xt{i}", [128, NB3, 1024], F32) for i in range(2)]
        hb = SB("hb", [128, NB3, 1024], BF16)
        hTs = [SB(f"hT{i}", [128, 8, NTK], BF16) for i in range(2)]
        yT = SB("yT", [128, 22, NTK], BF16)
        asb = [SB(f"asb{i}", [128, NTK + 2], F32) for i in range(2)]
        cc = [SB(f"cc{i}", [128, NTK], F32) for i in range(2)]
        gl = [SB(f"gl{i}", [128, NTK], F32) for i in range(2)]
        hist = SB("hist", [128, 22, 2], F32)
        junk = SB("junk", [128, 1024], BF16)
        ssq = SB("ssq", [128, 4], F32)
        rstd = SB("rstd", [128, 4], F32)
        nhalf = SB("nhalf", [128, 4], F32)
        gv = SB("gv", [128, 8], F32)
        cw = SB("cw", [128, 22, 3], F32)
        cb = SB("cb", [128, 22], F32)
        ident = SB("ident", [128, 128], BF16)
        gfb = SB("gfb", [128, 1024], F32)
        tp = [PS(f"tp{i}", [128, 8, 128], BF16) for i in range(2)]
        mm = [PS(f"mm{i}", [128, 512], F32) for i in range(6)]
        stages = [(xt[i][:, :, :].rearrange("p b f -> p (b f)"), [("xt", i, b) for b in range(NB3)], f"D_bxt{i}") for i in range(2)]

        em.op("pool", lambda: nc.gpsimd.memset(nhalf[:, :], -0.5), writes=["nhalf"])
        em.op("pool", lambda: nc.gpsimd.memset(hist[:, :, :], 0.0), writes=[("hist", fc) for fc in range(22)])
        em.dma("sp", [(gv[:, :], g2_d[:, :])], "D_gv", writes=["gvec"])
        em.dma("sp", [(cw[:, :, :], cw_d[:, :, :])], "D_cw", writes=["cw"])
        em.dma("sp", [(cb[:, :], cb_d[:, :])], "D_cb", writes=["cb"])
        em.dma("sp", [(ident[:, :], id_d[:, :])], "D_id", writes=["ident"])
        em.dma("sp", [(gfb[:, :], gf_d[:, :])], "D_gf", writes=["gfb"])
        emit_load_w(em, nc, "sp", wup_d, 8, [(0, 2816, 0, 1.0), (2816, 2816, 2816, 1.0)], Wup, gv, stages, "Wup")
        emit_load_w(em, nc, "sp", wdn_d, 22, [(0, 1024, 0, 1.0)], Wdn, None, stages, "Wdn")
        wup_res = [("Wup", kc, d0) for kc in range(8) for d0 in (0, 2816)]
        wdn_res = [("Wdn", kc, 0) for kc in range(22)]
        mmi = [0]

        def next_mm():
            i = mmi[0] % len(mm)
            mmi[0] += 1
            return mm[i], ("mm", i)

        def load_b(ti):
            s = ti % 2
            src = tile_view(xm_d, ti)
            em.dma("sp", [(xt[s][:, b, :], src[:, b, :]) for b in range(NB3)], f"D_bxt{s}",
                   reads=[("xmid", ti)], writes=[("xt", s, b) for b in range(NB3)])

        def norm_b(ti):
            s = ti % 2
            emit_norm_transpose(em, nc, xt[s], [("xt", s, b) for b in range(NB3)], hb, hTs[s], tp, ident, ssq, rstd, nhalf,
                                junk, ("nb", s), NB3)

        load_b(0)
        norm_b(0)
        for ti in range(NT):
            s = ti % 2
            if ti + 1 < NT:
                load_b(ti + 1)
            hT = hTs[s]
            tag = ("nb", s)
            xres = [("xt", s, b) for b in range(NB3)]
            hres = [(tag, "hT", b) for b in range(NB3)]
            for fc in range(22):
                j = fc % 2
                psa, pra = next_mm()
                for kc in range(8):
                    em.op("pe", lambda psa=psa, kc=kc, fc=fc: nc.tensor.matmul(
                        psa[:, 0:NTK], lhsT=Wup[:, kc, fc * 128:(fc + 1) * 128], rhs=hT[:, kc, :], start=(kc == 0), stop=(kc == 7)),
                        reads=(wup_res + hres) if kc == 0 else (), writes=[pra], sig=(kc == 7))
                psv, prv = next_mm()
                for kc in range(8):
                    em.op("pe", lambda psv=psv, kc=kc, fc=fc: nc.tensor.matmul(
                        psv[:, 0:NTK], lhsT=Wup[:, kc, D_FF + fc * 128:D_FF + (fc + 1) * 128], rhs=hT[:, kc, :], start=(kc == 0), stop=(kc == 7)),
                        reads=(wup_res + hres) if kc == 0 else (), writes=[prv], sig=(kc == 7))
                A = asb[j]
                ar = ("asb", j)
                em.op("pool", lambda A=A, fc=fc: nc.gpsimd.tensor_copy(out=A[:, 0:2], in_=hist[:, fc, :]),
                      reads=[("hist", fc)], writes=[(ar, "h")])
                em.op("act", lambda A=A, psa=psa: nc.scalar.copy(out=A[:, 2:NTK + 2], in_=psa[:, 0:NTK]),
                      reads=[pra], writes=[(ar, "m")])
                em.op("pool", lambda A=A, fc=fc: nc.gpsimd.tensor_copy(out=hist[:, fc, :], in_=A[:, NTK:NTK + 2]),
                      reads=[(ar, "m"), (ar, "h")], writes=[("hist", fc)])
                C = cc[j]
                cr = ("cc", j)
                em.op("pool", lambda A=A, C=C, fc=fc: nc.gpsimd.tensor_scalar(
                    out=C[:, :], in0=A[:, 2:NTK + 2], scalar1=cw[:, fc, 2:3], scalar2=cb[:, fc:fc + 1], op0=ALU.mult, op1=ALU.add),
                    reads=[(ar, "m"), "cw", "cb"], writes=[cr])
                em.op("dve", lambda A=A, C=C, fc=fc: nc.vector.scalar_tensor_tensor(
                    out=C[:, :], in0=A[:, 1:NTK + 1], scalar=cw[:, fc, 1:2], in1=C[:, :], op0=ALU.mult, op1=ALU.add),
                    reads=[(ar, "m"), (ar, "h"), "cw", cr], writes=[cr])
                em.op("dve", lambda A=A, C=C, fc=fc: nc.vector.scalar_tensor_tensor(
                    out=C[:, :], in0=A[:, 0:NTK], scalar=cw[:, fc, 0:1], in1=C[:, :], op0=ALU.mult, op1=ALU.add),
                    reads=[(ar, "m"), (ar, "h"), "cw", cr], writes=[cr])
                G = gl[j]
                gr = ("gl", j)
                em.op("act", lambda G=G, C=C: nc.scalar.activation(out=G[:, :], in_=C[:, :], func=AF.Erf, scale=0.7071067811865476),
                      reads=[cr], writes=[gr])
                em.op("dve", lambda G=G, C=C: nc.vector.scalar_tensor_tensor(
                    out=G[:, :], in0=G[:, :], scalar=1.0, in1=C[:, :], op0=ALU.add, op1=ALU.mult),
                    reads=[gr, cr], writes=[gr])
                em.op("dve", lambda G=G, psv=psv, fc=fc: nc.vector.scalar_tensor_tensor(
                    out=yT[:, fc, :], in0=G[:, :], scalar=0.5, in1=psv[:, 0:NTK], op0=ALU.mult, op1=ALU.mult),
                    reads=[gr, prv], writes=[("yT", fc)])
            yres = [("yT", fc) for fc in range(22)]
            if ti + 1 < NT:
                norm_b(ti + 1)
            for b in range(NB3):
                for half in range(2):
                    ps, pr = next_mm()
                    for fc in range(22):
                        em.op("pe", lambda ps=ps, fc=fc, b=b, half=half: nc.tensor.matmul(
                            ps[:, :], lhsT=yT[:, fc, b * 128:(b + 1) * 128], rhs=Wdn[:, fc, half * 512:(half + 1) * 512],
                            start=(fc == 0), stop=(fc == 21)),
                            reads=(wdn_res + yres) if fc == 0 else (), writes=[pr], sig=(fc == 21))
                    em.op("dve", lambda ps=ps, b=b, half=half: nc.vector.tensor_tensor(
                        out=xt[s][:, b, half * 512:(half + 1) * 512], in0=xt[s][:, b, half * 512:(half + 1) * 512], in1=ps[:, :], op=ALU.add),
                        reads=[pr, ("xt", s, b)], writes=[("xt", s, b)])
            if final:
                ftag = ("nf", s)
                for b in range(NB3):
                    em.op("act", lambda b=b: nc.scalar.activation(out=junk[:, :], in_=xt[s][:, b, :], func=AF.Square,
                                                                  accum_out=ssq[:, b:b + 1]),
                          reads=[("xt", s, b)], writes=[(tag[0], "junk"), (tag[0], "ssq", b)])
                em.op("pool", lambda: nc.gpsimd.tensor_scalar(out=rstd[:, 0:NB3], in0=ssq[:, 0:NB3], scalar1=1.0 / D_MODEL,
                                                              scalar2=RMS_EPS, op0=ALU.mult, op1=ALU.add),
                      reads=[(tag[0], "ssq", b) for b in range(NB3)], writes=[(tag[0], "rstd")])
                em.op("pool", lambda: nc.gpsimd.tensor_tensor(out=rstd[:, 0:NB3], in0=rstd[:, 0:NB3], in1=nhalf[:, 0:NB3], op=ALU.pow),
                      reads=[(tag[0], "rstd"), "nhalf"], writes=[(tag[0], "rstd")])
                for b in range(NB3):
                    em.op("dve", lambda b=b: nc.vector.scalar_tensor_tensor(
                        out=xt[s][:, b, :], in0=xt[s][:, b, :], scalar=rstd[:, b:b + 1], in1=gfb[:, :], op0=ALU.mult, op1=ALU.mult),
                        reads=[("xt", s, b), (tag[0], "rstd"), "gfb"], writes=[("xt", s, b)])
            dst = tile_view(xo_d, ti)
            em.dma("pool", [(dst[:, b, :], xt[s][:, b, :]) for b in range(NB3)], f"D_bxo{s}",
                   reads=xres, writes=[("xo", ti)])
        em.barrier()


DSW = ((128, 1), (512, 4), (2048, 16))
CH = 2048


def p2_consts(p):
    k = np.arange(128)[:, None]
    q = np.arange(128)[None, :]
    slopes = 2.0 ** (-8.0 * np.arange(1, 13) / 12)
    bias = np.zeros((128, 6, 512), np.float32)
    for g, (w, d) in enumerate(DSW):
        for j in range(2):
            sl = slopes[g * 4 + 2 * p + j]
            prev = np.where(k >= q, -sl * d * (q + 128 - k), NEG)
            cur = np.where(k <= q, -sl * d * (q - k), NEG)
            bias[:, g * 2 + j, :] = np.concatenate([prev, cur, prev, cur], axis=1)
    mtri = (q > k).astype(np.float32)
    ltri = -(k >= q).astype(np.float32)
    return {"biasA": bias, "mtri": _bf16(mtri), "ltri": _bf16(ltri), "nones": _bf16(-np.ones((128, 128)))}


def build_p2(S=SEQ):
    nc = bass.Bass("TRN2", target_bir_lowering=False)
    qkA_d = nc.dram_tensor("qkA", [3, 2, 128, S], BF16, kind="ExternalInput").ap()
    qkB_d = nc.dram_tensor("qkB", [2, 128, S], BF16, kind="ExternalInput").ap()
    vA_d = nc.dram_tensor("vA", [3, S, 128], BF16, kind="ExternalInput").ap()
    vB_d = nc.dram_tensor("vB", [S, 128], BF16, kind="ExternalInput").ap()
    bias_d = nc.dram_tensor("biasA", [128, 6, 512], F32, kind="ExternalInput").ap()
    mtri_d = nc.dram_tensor("mtri", [128, 128], BF16, kind="ExternalInput").ap()
    ltri_d = nc.dram_tensor("ltri", [128, 128], BF16, kind="ExternalInput").ap()
    nones_d = nc.dram_tensor("nones", [128, 128], BF16, kind="ExternalInput").ap()
    o_d = nc.dram_tensor("oT", [256, S], BF16, kind="ExternalOutput").ap()
    den_d = nc.dram_tensor("den_scr", [2, S], F32).ap()
    em = Em(nc)
    emit_p2(nc, em, "p2", lambda g: qkA_d[g, 0, :, :], lambda g: qkA_d[g, 1, :, :], lambda g: vA_d[g, :, :],
            qkB_d[0, :, :], qkB_d[1, :, :], vB_d, bias_d, mtri_d, ltri_d, nones_d, o_d, 0, 128, den_d, S)
    em.finish("sp")
    return nc


def emit_p2(nc, em, pfx, qA, kA, vA, qB_d, kB_d, vB_d, bias_d, mtri_d, ltri_d, nones_d, o_d, oa_row0, ob_row0, den_d, S):
    NQT = S // 512
    NKB = S // 128
    with ExitStack() as stk:
        SB = lambda name, shape, dt: stk.enter_context(nc.sbuf_tensor(pfx + "s_" + name, shape, dt))
        PS = lambda name, shape, dt: stk.enter_context(nc.psum_tensor(pfx + "p_" + name, shape, dt))
        QB = SB("QB", [128, S], BF16)
        KB = SB("KB", [128, S], BF16)
        VB = SB("VB", [128, NKB, 128], BF16)
        mtri = SB("mtri", [128, 128], BF16)
        ltri = SB("ltri", [128, 128], BF16)
        nones = SB("nones", [128, 128], BF16)
        Et = [SB(f"E{i}", [128, 512], F32) for i in range(3)]
        SPt = [SB(f"SP{i}", [128, 512], BF16) for i in range(4)]
        PBt = [SB(f"PB{i}", [128, 512], BF16) for i in range(3)]
        Spre = [SB(f"Spre{i}", [128, 512], BF16) for i in range(2)]
        obst = [SB(f"obst{i}", [64, 512], BF16) for i in range(2)]
        zb = [PS(f"z{i}", [128, 512], F32) for i in range(4)]
        po = [PS(f"po{i}", [64, 512], F32) for i in range(2)]
        QA = [SB(f"QA{i}", [128, CH], BF16) for i in range(2)]
        KA = [SB(f"KA{i}", [128, 2 * CH], BF16) for i in range(2)]
        VA = [SB(f"VA{i}", [128, 32, 2, 65], BF16) for i in range(2)]
        biasA = SB("biasA", [128, 6, 512], F32)
        TA = [SB(f"TA{i}", [128, 512], F32) for i in range(2)]
        PA = [SB(f"PA{i}", [128, 512], BF16) for i in range(2)]
        acc = [SB(f"acc{i}", [65, CH], F32) for i in range(2)]
        dbc = [SB(f"dbc{i}", [64, CH], F32) for i in range(2)]
        oast = [SB(f"oast{i}", [64, CH], BF16) for i in range(2)]
        sA = [zb[0], zb[1]]
        oA = [zb[2], zb[3]]

        em.dma("sp", [(mtri[:, :], mtri_d[:, :])], "D_c0", writes=["mtri"])
        em.dma("sp", [(ltri[:, :], ltri_d[:, :])], "D_c1", writes=["ltri"])
        em.dma("sp", [(nones[:, :], nones_d[:, :])], "D_c2", writes=["nones"])
        em.dma("sp", [(biasA[:, :, :], bias_d[:, :, :])], "D_c3", writes=["biasA"])
        em.dma("sp", [(QB[:, :], qB_d)], "D_QB", writes=["QB"])
        em.dma("sp", [(KB[:, :], kB_d)], "D_KB", writes=["KB"])
        em.dma("sp", [(VB[:, :, :], vB_d.rearrange("(n p) c -> p n c", p=128))], "D_VB", writes=["VB"])
        for i in range(2):
            em.op("pool", lambda i=i: nc.gpsimd.memset(VA[i][:, :, :, 64:65], 1.0), writes=[("VA", i)])

        steps = []
        per_head = [[(h, i, kb) for i in range(NQT) for kb in range(4 * i + 3, -1, -1)] for h in range(2)]
        for a, b in zip(*per_head):
            steps += [a, b]

        def geom(step):
            h, i, kb = step
            j = kb - 4 * i
            c0 = 128 * j if j >= 0 else 0
            return h, i, kb, j, c0

        NZ, NE, NSP, NPB = len(zb), len(Et), len(SPt), len(PBt)

        def s0_z(n):
            h, i, kb, j, c0 = geom(steps[n])
            z, zr = zb[n % NZ], ("z", n % NZ)
            hp = slice(h * 64, (h + 1) * 64)
            em.op("pe", lambda: nc.tensor.matmul(z[:, c0:512], lhsT=KB[hp, kb * 128:(kb + 1) * 128],
                                                 rhs=QB[hp, i * 512 + c0:(i + 1) * 512], start=True, stop=True),
                  reads=["QB", "KB"], writes=[zr])

        def s1_exp(n):
            h, i, kb, j, c0 = geom(steps[n])
            z, zr = zb[n % NZ], ("z", n % NZ)
            E, er = Et[n % NE], ("E", n % NE)
            em.op("act", lambda: nc.scalar.activation(out=E[:, c0:512], in_=z[:, c0:512], func=AF.Exp), reads=[zr], writes=[er])

        def s2_ln(n):
            h, i, kb, j, c0 = geom(steps[n])
            E, er = Et[n % NE], ("E", n % NE)
            SPn, sr = SPt[n % NSP], ("SP", n % NSP)
            em.op("act", lambda: nc.scalar.activation(out=SPn[:, c0:512], in_=E[:, c0:512], func=AF.Ln, bias=1.0, scale=1.0),
                  reads=[er], writes=[sr])
            if j >= 0:
                em.op("dve", lambda: nc.vector.tensor_tensor(out=SPn[:, c0:c0 + 128], in0=SPn[:, c0:c0 + 128], in1=mtri[:, :], op=ALU.mult),
                      reads=[sr, "mtri"], writes=[sr])

        def s3_cum(n):
            h, i, kb, j, c0 = geom(steps[n])
            z, zr = zb[n % NZ], ("z", n % NZ)
            SPn, sr = SPt[n % NSP], ("SP", n % NSP)
            SPR, spr = Spre[h], ("Spre", h)
            c1 = c0 + 128 if j >= 0 else 0
            if c1 < 512:
                em.op("pe", lambda: nc.tensor.matmul(z[:, c1:512], lhsT=nones[:, :], rhs=SPR[:, c1:512], start=False, stop=True,
                                                     skip_group_check=True),
                      reads=[spr, "nones"], writes=[zr])
            em.op("pe", lambda: nc.tensor.matmul(z[:, c0:512], lhsT=ltri[:, :], rhs=SPn[:, c0:512], start=False, stop=True,
                                                 skip_group_check=True),
                  reads=[sr, "ltri"], writes=[zr])
            if kb > 0:
                if j >= 0:
                    em.op("dve", lambda: nc.vector.tensor_copy(out=SPR[:, c0:c0 + 128], in_=SPn[:, c0:c0 + 128]), reads=[sr], writes=[spr])
                if c1 < 512:
                    em.op("dve", lambda: nc.vector.tensor_tensor(out=SPR[:, c1:512], in0=SPR[:, c1:512], in1=SPn[:, c1:512], op=ALU.add),
                          reads=[sr, spr], writes=[spr])

        def s4_p(n):
            h, i, kb, j, c0 = geom(steps[n])
            z, zr = zb[n % NZ], ("z", n % NZ)
            P, pr = PBt[n % NPB], ("PB", n % NPB)
            em.op("act", lambda: nc.scalar.activation(out=P[:, c0:512], in_=z[:, c0:512], func=AF.Exp), reads=[zr], writes=[pr])
            if j >= 0:
                if c0 > 0:
                    em.op("pool", lambda: nc.gpsimd.memset(P[:, 0:c0], 0.0), writes=[pr])
                em.op("dve", lambda: nc.vector.tensor_tensor(out=P[:, c0:c0 + 128], in0=P[:, c0:c0 + 128], in1=mtri[:, :], op=ALU.mult),
                      reads=[pr, "mtri"], writes=[pr])

        def s5_pv(n):
            h, i, kb, j, c0 = geom(steps[n])
            P, pr = PBt[n % NPB], ("PB", n % NPB)
            first = (kb == 4 * i + 3)
            em.op("pe", lambda: nc.tensor.matmul(po[h][:, :], lhsT=VB[:, kb, h * 64:(h + 1) * 64], rhs=P[:, :], start=first, stop=(kb == 0)),
                  reads=[pr, "VB"], writes=[("po", h)])
            if kb == 0:
                em.op("dve", lambda: nc.vector.tensor_copy(out=obst[h][:, :], in_=po[h][:, :]), reads=[("po", h)], writes=[("obst", h)])
                em.dma("pool", [(o_d[ob_row0 + h * 64:ob_row0 + (h + 1) * 64, i * 512:(i + 1) * 512], obst[h][:, :])], f"D_obst{h}",
                       reads=[("obst", h)], writes=[("ob_out", h, i)])

        def a_items():
            NCH = S // CH
            li = 0
            for c in range(NCH):
                for g, (w, d) in enumerate(DSW):
                    sl = li % 2
                    li += 1
                    nper = 16 // d
                    tok0 = CH * c
                    em.dma("sp", [(QA[sl][:, :], qA(g)[:, tok0:tok0 + CH])], f"D_QA{sl}", writes=[("QA", sl)])
                    kp = []
                    if c > 0:
                        kp.append((KA[sl][:, 0:CH], kA(g)[:, tok0 - CH:tok0]))
                    kp.append((KA[sl][:, CH:2 * CH], kA(g)[:, tok0:tok0 + CH]))
                    em.dma("sp", kp, f"D_KA{sl}", writes=[("KA", sl)])
                    vp = []
                    for r in range(d):
                        m_first = (tok0 // d) - 128
                        nn0 = 0
                        if c == 0:
                            m_first += 128
                            nn0 = 1
                        nsl = nper + 1 - nn0
                        src = vA(g)[m_first * d + r:(m_first + nsl * 128 - 1) * d + r + 1:d, :]
                        src = src.rearrange("(n p) (h c) -> p n h c", p=128, h=2)
                        base = r * (nper + 1) + nn0
                        for hh in range(2):
                            vp.append((VA[sl][:, base:base + nsl, hh, 0:64], src[:, :, hh, :]))
                    em.dma("sp", vp, f"D_VA{sl}", writes=[("VA", sl)])
                    yield
                    Kv = KA[sl][:, :].rearrange("p (m dd) -> p m dd", dd=d)
                    Qv = QA[sl][:, :].rearrange("p (m dd) -> p m dd", dd=d)
                    descs = []
                    blocks = [(r, nn) for r in range(d) for nn in range(nper)]
                    for j in range(2):
                        for bi in range(0, len(blocks), 2):
                            descs.append((j, blocks[bi:bi + 2]))

                    def hasp(nn):
                        return not (c == 0 and nn == 0)

                    def st_s(k):
                        j, pair = descs[k]
                        hp = slice(j * 64, (j + 1) * 64)
                        sa, sar = sA[k % 2], ("sA", k % 2)
                        for qi, (r, nn) in enumerate(pair):
                            mq = nn * 128
                            if hasp(nn):
                                em.op("pe", lambda qi=qi, mq=mq, r=r: nc.tensor.matmul(
                                    sa[:, qi * 256:qi * 256 + 128], lhsT=Kv[hp, CH // d + mq - 128:CH // d + mq, r],
                                    rhs=Qv[hp, mq:mq + 128, r], start=True, stop=True),
                                    reads=[("QA", sl), ("KA", sl)], writes=[sar], sig=False)
                            em.op("pe", lambda qi=qi, mq=mq, r=r: nc.tensor.matmul(
                                sa[:, qi * 256 + 128:qi * 256 + 256], lhsT=Kv[hp, CH // d + mq:CH // d + mq + 128, r],
                                rhs=Qv[hp, mq:mq + 128, r], start=True, stop=True),
                                reads=[("QA", sl), ("KA", sl)], writes=[sar], sig=(qi == len(pair) - 1))

                    def st_e(k):
                        j, pair = descs[k]
                        sa, sar = sA[k % 2], ("sA", k % 2)
                        T, tr = TA[k % 2], ("TA", k % 2)
                        PAt, par = PA[k % 2], ("PA", k % 2)
                        lo = 0 if hasp(pair[0][1]) else 128
                        hi = 256 * len(pair)
                        em.op("dve", lambda: nc.vector.tensor_tensor(out=T[:, lo:hi], in0=sa[:, lo:hi], in1=biasA[:, g * 2 + j, lo:hi], op=ALU.add),
                              reads=[sar, "biasA"], writes=[tr])
                        em.op("act", lambda: nc.scalar.activation(out=PAt[:, lo:hi], in_=T[:, lo:hi], func=AF.Exp),
                              reads=[tr], writes=[par])

                    def st_o(k):
                        j, pair = descs[k]
                        PAt, par = PA[k % 2], ("PA", k % 2)
                        oa, oar = oA[k % 2], ("oA", k % 2)
                        accv = acc[j][:, :].rearrange("p (m dd) -> p m dd", dd=d)
                        for qi, (r, nn) in enumerate(pair):
                            slot = r * (nper + 1) + nn
                            last = (qi == len(pair) - 1)
                            if hasp(nn):
                                em.op("pe", lambda qi=qi, slot=slot: nc.tensor.matmul(
                                    oa[0:65, qi * 128:(qi + 1) * 128], lhsT=VA[sl][:, slot, j, :], rhs=PAt[:, qi * 256:qi * 256 + 128],
                                    start=True, stop=False), reads=[par, ("VA", sl)], writes=[oar], sig=False)
                            em.op("pe", lambda qi=qi, slot=slot, nn=nn: nc.tensor.matmul(
                                oa[0:65, qi * 128:(qi + 1) * 128], lhsT=VA[sl][:, slot + 1, j, :], rhs=PAt[:, qi * 256 + 128:qi * 256 + 256],
                                start=(not hasp(nn)), stop=True), reads=[par, ("VA", sl)], writes=[oar], sig=last)
                        for qi, (r, nn) in enumerate(pair):
                            dst = accv[:, nn * 128:(nn + 1) * 128, r]
                            if g == 0:
                                em.op("dve", lambda qi=qi, dst=dst: nc.vector.tensor_copy(out=dst, in_=oa[0:65, qi * 128:(qi + 1) * 128]),
                                      reads=[oar], writes=[("acc", j)])
                            else:
                                em.op("dve", lambda qi=qi, dst=dst: nc.vector.tensor_tensor(out=dst, in0=dst, in1=oa[0:65, qi * 128:(qi + 1) * 128], op=ALU.add),
                                      reads=[oar, ("acc", j)], writes=[("acc", j)])

                    ND = len(descs)
                    for k in range(-2, ND):
                        if 0 <= k + 2 < ND:
                            st_s(k + 2)
                        if 0 <= k + 1 < ND:
                            st_e(k + 1)
                        if 0 <= k < ND:
                            st_o(k)
                    yield
                for j in range(2):
                    em.dma("pool", [(den_d[j:j + 1, tok0:tok0 + CH], acc[j][64:65, :])], f"D_den{j}", reads=[("acc", j)], writes=[("den", j)])
                    em.dma("pool", [(dbc[j][:, :], den_d[j:j + 1, tok0:tok0 + CH].partition_broadcast(64))], f"D_dbc{j}",
                           reads=[("den", j)], writes=[("dbc", j)])
                    em.op("dve", lambda j=j: nc.vector.reciprocal(out=dbc[j][:, :], in_=dbc[j][:, :]), reads=[("dbc", j)], writes=[("dbc", j)])
                    em.op("dve", lambda j=j: nc.vector.tensor_tensor(out=oast[j][:, :], in0=acc[j][0:64, :], in1=dbc[j][:, :], op=ALU.mult),
                          reads=[("dbc", j), ("acc", j)], writes=[("oast", j)])
                    em.dma("pool", [(o_d[oa_row0 + j * 64:oa_row0 + (j + 1) * 64, tok0:tok0 + CH], oast[j][:, :])], f"D_oast{j}",
                           reads=[("oast", j)], writes=[("oa_out", j, c)])
                    yield

        NS = len(steps)
        agen = a_items()
        A_EVERY = 10 ** 9
        for n in range(-3, NS):
            if 0 <= n + 3 < NS:
                s0_z(n + 3)
            if 0 <= n + 2 < NS:
                s1_exp(n + 2)
            if 0 <= n + 1 < NS:
                s2_ln(n + 1)
                s3_cum(n + 1)
            if 0 <= n < NS:
                s4_p(n)
                s5_pv(n)
            if n >= 0 and n % A_EVERY == 0:
                next(agen, None)
        em.barrier()
        for _ in agen:
            pass
        em.barrier()


def _pc(v, n):
    return np.ascontiguousarray(np.asarray(v, np.float32).reshape(n, 128).T)


def _run(nc, in_maps):
    res = run_bass_kernel_spmd(nc, in_maps, core_ids=list(range(NCORES)))
    return res.results


def kernel(x, norm1, w_in, b_gate, w_br, w_o, norm2, w_up, conv_w, conv_b, w_down, norm_f):
    f32 = lambda a: np.ascontiguousarray(np.asarray(a, dtype=np.float32))
    x = f32(x)
    norm1, w_in, b_gate, w_br, w_o = f32(norm1), f32(w_in), f32(b_gate), f32(w_br), f32(w_o)
    norm2, w_up, conv_w, conv_b, w_down, norm_f = f32(norm2), f32(w_up), f32(conv_w), f32(conv_b), f32(w_down), f32(norm_f)
    ident = _bf16(np.eye(128))
    nc1 = build_p1()
    nc2 = build_p2()
    consts = [p2_consts(p) for p in range(2)]
    gfb = np.ascontiguousarray(np.broadcast_to(norm_f, (128, D_MODEL)))
    xcur = [x[c // 2, (c % 2) * TOK:(c % 2 + 1) * TOK] for c in range(NCORES)]
    for l in range(DEPTH):
        wqkv = np.ascontiguousarray(w_in[l][:, :3072])
        g1 = _pc(norm1[l], 8)
        r1 = _run(nc1, [{"x": np.ascontiguousarray(xcur[c]), "w": wqkv, "g": g1, "ident": ident} for c in range(NCORES)])
        ins2 = []
        for c in range(NCORES):
            b, p = c // 2, c % 2
            qk = [np.asarray(r1[2 * b + h]["qk"]) for h in range(2)]
            v = [np.asarray(r1[2 * b + h]["v"]) for h in range(2)]
            cat = lambda ch: np.concatenate([qk[0][ch], qk[1][ch]], axis=1)
            qkA = np.stack([np.stack([cat(2 * g + p), cat(6 + 2 * g + p)]) for g in range(3)])
            qkB = np.stack([cat(12 + p), cat(14 + p)])
            vcat = np.concatenate(v, axis=0)
            vA = np.stack([vcat[:, (g * 4 + 2 * p) * 64:(g * 4 + 2 * p) * 64 + 128] for g in range(3)])
            vB = vcat[:, 768 + 2 * p * 64:768 + 2 * p * 64 + 128]
            ins2.append(dict(qkA=np.ascontiguousarray(qkA), qkB=np.ascontiguousarray(qkB), vA=np.ascontiguousarray(vA),
                             vB=np.ascontiguousarray(vB), **consts[p]))
        r2 = _run(nc2, ins2)
        nc3 = build_p3(l == DEPTH - 1)
        ins3 = []
        for c in range(NCORES):
            b, h = c // 2, c % 2
            o0, o1 = np.asarray(r2[2 * b]["oT"]), np.asarray(r2[2 * b + 1]["oT"])
            oT = np.concatenate([o0[0:128], o1[0:128], o0[128:256], o1[128:256]], axis=0)
            xin = np.zeros((TOKP, D_MODEL), np.float32)
            oin = np.zeros((512, TOKP), dtype=oT.dtype)
            xin[128:] = xcur[c]
            oin[:, 128:] = oT[:, h * TOK:(h + 1) * TOK]
            if h == 1:
                xin[:128] = xcur[c - 1][TOK - 128:]
                oin[:, :128] = oT[:, TOK - 128:TOK]
            ins3.append(dict(x=xin, oT=oin, wg=np.ascontiguousarray(w_in[l][:, 3072:]), bg=_pc(b_gate[l], 16), g1=g1,
                             wbr=w_br[l], wo=w_o[l], g2=_pc(norm2[l], 8), wup=w_up[l],
                             cw=np.ascontiguousarray(conv_w[l].reshape(3, 22, 128).transpose(2, 1, 0)),
                             cb=_pc(conv_b[l], 22), wdn=w_down[l], gf=gfb, ident=ident))
        r3 = _run(nc3, ins3)
        xcur = [np.asarray(r3[c]["xo"])[128:] for c in range(NCORES)]
    out = np.empty((BATCH, SEQ, D_MODEL), np.float32)
    for c in range(NCORES):
        out[c // 2, (c % 2) * TOK:(c % 2 + 1) * TOK] = xcur[c]
    return out


NTOKF = 8448
LNAMES = ["wqkv", "g1", "wg", "bg", "wbr", "wo", "g2", "wup", "cw", "cb", "wdn"]
LSHAPES = {"wqkv": [D_MODEL, 3072], "g1": [128, 8], "wg": [D_MODEL, 2048], "bg": [128, 16], "wbr": [512, D_MODEL],
           "wo": [D_MODEL, D_MODEL], "g2": [128, 8], "wup": [D_MODEL, 2 * D_FF], "cw": [128, 22, 3], "cb": [128, 22],
           "wdn": [D_FF, D_MODEL]}


def build_fused():
    nc = bass.Bass("TRN2", target_bir_lowering=False)
    ext = lambda name, shape, dt: nc.dram_tensor(name, shape, dt, kind="ExternalInput").ap()
    x_d = ext("x", [NTOKF, D_MODEL], F32)
    W = [{n: ext(f"{n}{l}", LSHAPES[n], F32) for n in LNAMES} for l in range(DEPTH)]
    gf_d = ext("gf", [128, D_MODEL], F32)
    id_d = ext("ident", [128, 128], BF16)
    bias_d = [ext(f"biasA{p}", [128, 6, 512], F32) for p in range(2)]
    mtri_d = ext("mtri", [128, 128], BF16)
    ltri_d = ext("ltri", [128, 128], BF16)
    nones_d = ext("nones", [128, 128], BF16)
    xo_d = nc.dram_tensor("xo", [NTOKF, D_MODEL], F32, kind="ExternalOutput").ap()
    qk = nc.dram_tensor("qk_scr", [16, 128, SEQ], BF16).ap()
    v = nc.dram_tensor("v_scr", [SEQ, 1024], BF16).ap()
    oT = nc.dram_tensor("oT_scr", [512, NTOKF], BF16).ap()
    den = nc.dram_tensor("den_scr", [2, SEQ], F32).ap()
    xmid = nc.dram_tensor("xmid_scr", [NTOKF, D_MODEL], F32).ap()
    x1 = nc.dram_tensor("x1_scr", [NTOKF, D_MODEL], F32).ap()
    em = Em(nc)
    for l in range(DEPTH):
        xin = x_d if l == 0 else x1
        xout = x1 if l < DEPTH - 1 else xo_d
        w = W[l]
        emit_p1(nc, em, f"L{l}a", xin[0:SEQ, :], w["wqkv"], w["g1"], id_d, qk, v, SEQ)
        for p in range(2):
            emit_p2(nc, em, f"L{l}b{p}",
                    lambda g, p=p: qk[2 * g + p, :, :], lambda g, p=p: qk[6 + 2 * g + p, :, :],
                    lambda g, p=p: v[:, (g * 4 + 2 * p) * 64:(g * 4 + 2 * p) * 64 + 128],
                    qk[12 + p, :, :], qk[14 + p, :, :], v[:, 768 + 2 * p * 64:768 + 2 * p * 64 + 128],
                    bias_d[p], mtri_d, ltri_d, nones_d, oT, p * 128, 256 + p * 128, den, SEQ)
        emit_p3(nc, em, f"L{l}c", l == DEPTH - 1, xin, oT, w["wg"], w["bg"], w["g1"], w["wbr"], w["wo"], w["g2"], w["wup"],
                w["cw"], w["cb"], w["wdn"], gf_d, id_d, xout, xmid, NTOKF)
    em.finish("sp")
    return nc


ACTIVE_CORES = [0, 1, 4, 5]


def kernel_fused(x, norm1, w_in, b_gate, w_br, w_o, norm2, w_up, conv_w, conv_b, w_down, norm_f):
    f32 = lambda a: np.ascontiguousarray(np.asarray(a, dtype=np.float32))
    x = f32(x)
    norm1, w_in, b_gate, w_br, w_o = f32(norm1), f32(w_in), f32(b_gate), f32(w_br), f32(w_o)
    norm2, w_up, conv_w, conv_b, w_down, norm_f = f32(norm2), f32(w_up), f32(conv_w), f32(conv_b), f32(w_down), f32(norm_f)
    shared = {"gf": np.ascontiguousarray(np.broadcast_to(norm_f, (128, D_MODEL))), "ident": _bf16(np.eye(128))}
    for p in range(2):
        c = p2_consts(p)
        shared[f"biasA{p}"] = c["biasA"]
        shared.update({k: c[k] for k in ("mtri", "ltri", "nones")})
    for l in range(DEPTH):
        shared.update({f"wqkv{l}": np.ascontiguousarray(w_in[l][:, :3072]), f"g1{l}": _pc(norm1[l], 8),
                       f"wg{l}": np.ascontiguousarray(w_in[l][:, 3072:]), f"bg{l}": _pc(b_gate[l], 16),
                       f"wbr{l}": w_br[l], f"wo{l}": w_o[l], f"g2{l}": _pc(norm2[l], 8), f"wup{l}": w_up[l],
                       f"cw{l}": np.ascontiguousarray(conv_w[l].reshape(3, 22, 128).transpose(2, 1, 0)),
                       f"cb{l}": _pc(conv_b[l], 22), f"wdn{l}": w_down[l]})
    nc = build_fused()
    const_keys = ("ident", "mtri", "ltri", "nones", "biasA0", "biasA1", "gf")
    zeros = {k: (v if k in const_keys else np.zeros_like(v)) for k, v in shared.items()}
    ins = []
    for c in range(NCORES):
        xin = np.zeros((NTOKF, D_MODEL), np.float32)
        if c in ACTIVE_CORES:
            xin[:SEQ] = x[ACTIVE_CORES.index(c)]
            ins.append(dict(shared, x=xin))
        else:
            ins.append(dict(zeros, x=xin))
    res = _run(nc, ins)
    out = np.empty((BATCH, SEQ, D_MODEL), np.float32)
    for b in range(BATCH):
        out[b] = np.asarray(res[ACTIVE_CORES[b]]["xo"])[:SEQ]
    return out
```
